# Optimizing a Trainium2 kernel written in Bass

```python
import jax, jax.numpy as jnp
from jax import lax
import numpy as np

D_MODEL = 4096
BATCH = 4
SEQ = 4096
DEPTH = 1

GRID_W = 64
CTX_LEN = 256
GLA_VAL_W = D_MODEL // 2
GLA_HEADS = 8
GLA_DV = GLA_VAL_W // GLA_HEADS
GLA_DK = GLA_DV // 2
GLA_KEY_W = GLA_HEADS * GLA_DK
GLA_CHUNK = 64
GLA_LOWRANK = 16
GLA_TAU = 16.0
ROPE_BASE = 10000.0
SG_WIDTH = D_MODEL - GLA_VAL_W
SG_GROUPS = 4
SG_GROUP_W = SG_WIDTH // SG_GROUPS
SG_CHUNK = 128
MIX_W = GLA_VAL_W + SG_WIDTH
D_FF = 4 * D_MODEL
N_MOD = 6
EPS = 1e-6
Q0 = 0
K0 = Q0 + GLA_KEY_W
V0 = K0 + GLA_KEY_W
R0 = V0 + GLA_VAL_W
LF0 = R0 + GLA_VAL_W
LB0 = LF0 + GLA_LOWRANK
SG0 = LB0 + GLA_LOWRANK
IN_COLS = SG0 + 2 * SG_WIDTH

kernel_name = "hybrid_gla_gmlp_prefix_dit_block"


def rmsnorm(t, g):
    tf = t.astype(jnp.float32)
    y = tf * lax.rsqrt(jnp.mean(tf * tf, axis=-1, keepdims=True) + EPS)
    return (y * g.astype(jnp.float32)).astype(t.dtype)


def layernorm(t, g, b):
    tf = t.astype(jnp.float32)
    mu = jnp.mean(tf, axis=-1, keepdims=True)
    var = jnp.mean(jnp.square(tf - mu), axis=-1, keepdims=True)
    y = (tf - mu) * lax.rsqrt(var + EPS)
    return (y * g.astype(jnp.float32) + b.astype(jnp.float32)).astype(t.dtype)


def modulate(h, shift, scale):
    return h * (1.0 + scale) + shift


def split_heads(t, d):
    return t.reshape(t.shape[:-1] + (GLA_HEADS, d))


def flip_seq(t):
    return jnp.flip(t, axis=1)


def rope_axis(t, pos):
    m = t.shape[-1] // 2
    inv_freq = ROPE_BASE ** (-jnp.arange(m, dtype=jnp.float32) / m)
    ang = pos.astype(jnp.float32)[:, None] * inv_freq[None, :]
    cos = jnp.cos(ang)[:, None, :]
    sin = jnp.sin(ang)[:, None, :]
    t1 = t[..., :m].astype(jnp.float32)
    t2 = t[..., m:].astype(jnp.float32)
    return jnp.concatenate([t1 * cos - t2 * sin, t1 * sin + t2 * cos], axis=-1).astype(t.dtype)


def rope2d(t, row_pos, col_pos):
    half = t.shape[-1] // 2
    return jnp.concatenate([rope_axis(t[..., :half], row_pos), rope_axis(t[..., half:], col_pos)], axis=-1)


def gla_qk(z):
    q = split_heads(z[..., Q0:K0], GLA_DK) * (GLA_DK ** -0.5)
    k = split_heads(z[..., K0:V0], GLA_DK)
    return q, k


def gla_log_decay(lr, w_dec, b_dec):
    a = (lr @ w_dec + b_dec).astype(jnp.float32)
    return split_heads(jax.nn.log_sigmoid(a) / GLA_TAU, GLA_DK)


def gla_chunked(q, k, v, log_a, s0):
    bsz, n, h, dk = q.shape
    dv = v.shape[-1]
    nc = n // GLA_CHUNK

    def to_chunks(t):
        return t.astype(jnp.float32).reshape(bsz, nc, GLA_CHUNK, h, t.shape[-1]).transpose(1, 0, 3, 2, 4)

    mask = jnp.tril(jnp.ones((GLA_CHUNK, GLA_CHUNK), dtype=bool))[None, None, :, :, None]

    def step(s, inp):
        qi, ki, vi, ai = inp
        b = jnp.cumsum(ai, axis=2)
        inter = jnp.einsum('bhck,bhkv->bhcv', qi * jnp.exp(b), s)
        diff = b[:, :, :, None, :] - b[:, :, None, :, :]
        decay = jnp.exp(jnp.where(mask, diff, -jnp.inf))
        att = jnp.einsum('bhik,bhjk,bhijk->bhij', qi, ki, decay)
        intra = jnp.einsum('bhij,bhjv->bhiv', att, vi)
        b_last = b[:, :, -1, :]
        s_new = jnp.exp(b_last)[..., None] * s + jnp.einsum(
            'bhck,bhcv->bhkv', ki * jnp.exp(b_last[:, :, None, :] - b), vi)
        return s_new, inter + intra

    s_fin, o = lax.scan(step, s0.astype(jnp.float32), (to_chunks(q), to_chunks(k), to_chunks(v), to_chunks(log_a)))
    o = o.transpose(1, 0, 3, 2, 4).reshape(bsz, n, h, dv)
    return o, s_fin


def gla_final_state(k, v, log_a):
    b = jnp.cumsum(log_a.astype(jnp.float32), axis=1)
    w = jnp.exp(b[:, -1:] - b)
    return jnp.einsum('bnhk,bnhv->bhkv', k.astype(jnp.float32) * w, v.astype(jnp.float32))


def gla_bidir(q, k, v, la_f, la_b, s_f0, s_b0):
    o_f, s_f = gla_chunked(q, k, v, la_f, s_f0)
    o_b, s_b = gla_chunked(flip_seq(q), flip_seq(k), flip_seq(v), flip_seq(la_b), s_b0)
    return o_f + flip_seq(o_b), s_f, s_b


def gla_readout(o, r, g):
    bsz, n = o.shape[0], o.shape[1]
    y = rmsnorm(o, g).astype(r.dtype).reshape(bsz, n, GLA_VAL_W)
    return y * jax.nn.silu(r)


def spatial_gating(zs, ln_g, ln_b, w_s, b_s):
    zs = jax.nn.gelu(zs, approximate=False)
    u, vv = zs[..., :SG_WIDTH], zs[..., SG_WIDTH:]
    vv = layernorm(vv, ln_g, ln_b)
    bsz, n = vv.shape[0], vv.shape[1]
    vv = vv.reshape(bsz, n // SG_CHUNK, SG_CHUNK, SG_GROUPS, SG_GROUP_W)
    s = jnp.einsum('gij,bnjgc->bnigc', w_s, vv) + b_s.T[:, :, None]
    return u * s.reshape(bsz, n, SG_WIDTH)


def token_mix(z, q, k, s_f0, s_b0, w_dec_f, b_dec_f, w_dec_b, b_dec_b,
              gla_norm_g, sg_ln_g, sg_ln_b, w_s, b_s, w_o):
    v = split_heads(z[..., V0:R0], GLA_DV)
    la_f = gla_log_decay(z[..., LF0:LB0], w_dec_f, b_dec_f)
    la_b = gla_log_decay(z[..., LB0:SG0], w_dec_b, b_dec_b)
    o, s_f, s_b = gla_bidir(q, k, v, la_f, la_b, s_f0, s_b0)
    y_gla = gla_readout(o, z[..., R0:LF0], gla_norm_g)
    y_sg = spatial_gating(z[..., SG0:], sg_ln_g, sg_ln_b, w_s, b_s)
    y = jnp.concatenate([y_gla, y_sg], axis=-1) @ w_o
    return y, s_f, s_b


def ctx_states(hc, w_in, w_dec_f, b_dec_f, w_dec_b, b_dec_b):
    k = split_heads(hc @ w_in[:, K0:V0], GLA_DK)
    v = split_heads(hc @ w_in[:, V0:R0], GLA_DV)
    lr = hc @ w_in[:, LF0:SG0]
    la_f = gla_log_decay(lr[..., :GLA_LOWRANK], w_dec_f, b_dec_f)
    la_b = gla_log_decay(lr[..., GLA_LOWRANK:], w_dec_b, b_dec_b)
    s_f = gla_final_state(k, v, la_f)
    s_b = gla_final_state(flip_seq(k), flip_seq(v), flip_seq(la_b))
    return s_f, s_b


def sq_relu_mlp(h, w_1, w_2):
    return jnp.square(jax.nn.relu(h @ w_1)) @ w_2


def setup_inputs(seed: int = 0) -> dict:
    key = jax.random.key(seed)
    ks = jax.random.split(key, 24)

    def nrm(k, shape, scale):
        return jax.random.normal(k, shape, jnp.float32) * scale

    def gain(k, shape):
        return 1.0 + nrm(k, shape, 0.01)

    L = DEPTH
    return {
        "x": nrm(ks[0], (BATCH, SEQ, D_MODEL), 1.0),
        "c": nrm(ks[1], (BATCH, D_MODEL), 1.0),
        "ctx": nrm(ks[2], (BATCH, CTX_LEN, D_MODEL), 1.0),
        "c_ctx": nrm(ks[3], (D_MODEL,), 1.0),
        "w_ada": nrm(ks[4], (L, D_MODEL, N_MOD * D_MODEL), D_MODEL ** -0.5),
        "b_ada": nrm(ks[5], (L, N_MOD * D_MODEL), 0.01),
        "pre1_g": gain(ks[6], (L, D_MODEL)),
        "post1_g": gain(ks[7], (L, D_MODEL)),
        "pre2_g": gain(ks[8], (L, D_MODEL)),
        "post2_g": gain(ks[9], (L, D_MODEL)),
        "w_in": nrm(ks[10], (L, D_MODEL, IN_COLS), D_MODEL ** -0.5),
        "w_dec_f": nrm(ks[11], (L, GLA_LOWRANK, GLA_KEY_W), GLA_LOWRANK ** -0.5),
        "b_dec_f": nrm(ks[12], (L, GLA_KEY_W), 0.1),
        "w_dec_b": nrm(ks[13], (L, GLA_LOWRANK, GLA_KEY_W), GLA_LOWRANK ** -0.5),
        "b_dec_b": nrm(ks[14], (L, GLA_KEY_W), 0.1),
        "gla_norm_g": gain(ks[15], (L, GLA_HEADS, GLA_DV)),
        "sg_ln_g": gain(ks[16], (L, SG_WIDTH)),
        "sg_ln_b": nrm(ks[17], (L, SG_WIDTH), 0.01),
        "w_s": nrm(ks[18], (L, SG_GROUPS, SG_CHUNK, SG_CHUNK), SG_CHUNK ** -0.5),
        "b_s": gain(ks[19], (L, SG_GROUPS, SG_CHUNK)),
        "w_o": nrm(ks[20], (L, MIX_W, D_MODEL), MIX_W ** -0.5),
        "w_1": nrm(ks[21], (L, D_MODEL, D_FF), D_MODEL ** -0.5),
        "w_2": nrm(ks[22], (L, D_FF, D_MODEL), D_FF ** -0.5),
    }


def reference(x, c, ctx, c_ctx, w_ada, b_ada, pre1_g, post1_g, pre2_g, post2_g, w_in,
              w_dec_f, b_dec_f, w_dec_b, b_dec_b, gla_norm_g, sg_ln_g, sg_ln_b, w_s, b_s,
              w_o, w_1, w_2):
    bsz, n = x.shape[0], x.shape[1]
    ROWS = n // GRID_W
    pos = jnp.arange(ROWS * GRID_W)
    row_pos = pos // GRID_W
    col_pos = pos % GRID_W
    zero_state = jnp.zeros((bsz, GLA_HEADS, GLA_DK, GLA_DV), jnp.float32)
    cond_x = jax.nn.silu(c)[:, None, :]
    cond_c = jax.nn.silu(c_ctx)

    for l in range(DEPTH):
        mod_x = cond_x @ w_ada[l] + b_ada[l]
        sh1, sc1, g1, sh2, sc2, g2 = jnp.split(mod_x, N_MOD, axis=-1)
        hx = modulate(rmsnorm(x, pre1_g[l]), sh1, sc1)

        if l == DEPTH - 1:
            mod_c = cond_c @ w_ada[l][:, :2 * D_MODEL] + b_ada[l][:2 * D_MODEL]
            csh1, csc1 = jnp.split(mod_c, 2, axis=-1)
            hc = modulate(rmsnorm(ctx, pre1_g[l]), csh1, csc1)
            s_f, s_b = ctx_states(hc, w_in[l], w_dec_f[l], b_dec_f[l], w_dec_b[l], b_dec_b[l])
        else:
            mod_c = cond_c @ w_ada[l] + b_ada[l]
            csh1, csc1, cg1, csh2, csc2, cg2 = jnp.split(mod_c, N_MOD, axis=-1)
            hc = modulate(rmsnorm(ctx, pre1_g[l]), csh1, csc1)
            zc = hc @ w_in[l]
            qc, kc = gla_qk(zc)
            mix_c, s_f, s_b = token_mix(zc, qc, kc, zero_state, zero_state,
                                        w_dec_f[l], b_dec_f[l], w_dec_b[l], b_dec_b[l],
                                        gla_norm_g[l], sg_ln_g[l], sg_ln_b[l], w_s[l], b_s[l], w_o[l])
            ctx = ctx + cg1 * rmsnorm(mix_c, post1_g[l])
            hc2 = modulate(rmsnorm(ctx, pre2_g[l]), csh2, csc2)
            ctx = ctx + cg2 * rmsnorm(sq_relu_mlp(hc2, w_1[l], w_2[l]), post2_g[l])

        zx = hx @ w_in[l]
        qx, kx = gla_qk(zx)
        qx = rope2d(qx, row_pos, col_pos)
        kx = rope2d(kx, row_pos, col_pos)
        mix_x, _, _ = token_mix(zx, qx, kx, s_f, s_b,
                                w_dec_f[l], b_dec_f[l], w_dec_b[l], b_dec_b[l],
                                gla_norm_g[l], sg_ln_g[l], sg_ln_b[l], w_s[l], b_s[l], w_o[l])
        x = x + g1 * rmsnorm(mix_x, post1_g[l])
        h2 = modulate(rmsnorm(x, pre2_g[l]), sh2, sc2)
        x = x + g2 * rmsnorm(sq_relu_mlp(h2, w_1[l], w_2[l]), post2_g[l])
    return x
```

```python
import os
import numpy as np
from contextlib import ExitStack
import concourse.bass as bass
import concourse.mybir as mybir
from concourse.bass_utils import run_bass_kernel_spmd

F32 = mybir.dt.float32
BF16 = mybir.dt.bfloat16
AF = mybir.ActivationFunctionType
ALU = mybir.AluOpType
AX = mybir.AxisListType

D = 4096
NT = int(os.environ.get("KSIM_NT", "2048"))
NPRE = NT
NCH = NT // 128
NCTX = 256
T = 512
KC = D // 128
DFF = 4 * D
IN_COLS = 10272
Q0, K0, V0, R0, LF0, LB0, SG0 = 0, 1024, 2048, 4096, 6144, 6160, 6176
U0, VV0 = SG0, SG0 + 2048
EPS = 1e-6
SEM_LIMIT = 24000
NWB = 2


class Res:
    __slots__ = ("name", "writers", "readers")

    def __init__(self, name):
        self.name = name
        self.writers = []
        self.readers = []


class Op:
    __slots__ = ("eng", "fn", "deps", "key", "ticket", "sem", "signal", "idx")


class Prog:
    ENGS = ("pe", "act", "dve", "pool", "sp")

    def __init__(self):
        self.ops = {e: [] for e in self.ENGS}
        self.keys = {}
        self.nops = 0
        self.pending = {e: [] for e in self.ENGS}
        self.lastc = {}

    def barrier(self):
        last = [op for op in self.lastc.values()]
        for k, lst in self.keys.items():
            last.append(lst[-1])
        for e in self.ENGS:
            self.pending[e] = list(last)

    def add(self, eng, fn, reads=(), writes=(), parts=(), key=None):
        op = Op()
        op.eng = eng
        op.fn = fn
        op.key = key
        op.signal = False
        op.ticket = None
        op.sem = None
        op.idx = self.nops
        self.nops += 1
        raw, other = [], []
        for r in reads:
            raw.extend(r.writers)
        for r in writes:
            other.extend(r.writers)
            other.extend(r.readers)
        for r in parts:
            if r.readers:
                other.extend(r.readers)
                other.extend(r.writers)
            else:
                for w in r.writers:
                    same = (w.key is None and key is None and w.eng == eng and eng == 'pe') or (w.key is not None and w.key == key)
                    if not same:
                        other.append(w)
        deps = {}
        same_ok = eng in ("act", "dve", "pool")

        def consider(d, is_raw):
            if d is op:
                return
            if d.key is None and d.eng == eng and key is None:
                if not same_ok:
                    return
            k = ("k", d.key) if d.key is not None else ("e", d.eng)
            cur = deps.get(k)
            if cur is None or d.idx > cur.idx:
                deps[k] = d

        for d in raw:
            consider(d, True)
        for d in other:
            consider(d, False)
        if self.pending[eng]:
            for d in self.pending[eng]:
                if not (d.key is None and d.eng == eng):
                    consider(d, True)
            self.pending[eng] = []
        op.deps = list(deps.values())
        for d in op.deps:
            d.signal = True
        for r in reads:
            r.readers.append(op)
        for r in writes:
            r.writers = [op]
            r.readers = []
        for r in parts:
            if r.readers:
                r.writers = [op]
                r.readers = []
            else:
                r.writers.append(op)
        self.ops[eng].append(op)
        if key is not None:
            self.keys.setdefault(key, []).append(op)
        elif fn is not None:
            self.lastc[eng] = op
        return op

    def emit(self, nc, stack):
        def newsem(nm):
            return stack.enter_context(nc.semaphore(nm))

        for e in self.ENGS:
            cnt, epoch, cur = 0, 0, None
            for op in self.ops[e]:
                if op.key is not None or not op.signal:
                    continue
                if cur is None or cnt >= SEM_LIMIT:
                    cur = newsem(f"s_{e}_{epoch}")
                    epoch += 1
                    cnt = 0
                cnt += 1
                op.sem = cur
                op.ticket = cnt
        for k, lst in self.keys.items():
            s = newsem(f"k_{k}")
            cnt = 0
            for op in lst:
                cnt += 16
                op.sem = s
                op.ticket = cnt
        handles = {"pe": "tensor", "act": "scalar", "dve": "vector", "pool": "gpsimd", "sp": "sync"}
        block = stack.enter_context(nc.Block())

        def make(e):
            oplist = self.ops[e]

            def body(eng):
                waited = {}
                for op in oplist:
                    for d in op.deps:
                        sid = id(d.sem)
                        if waited.get(sid, 0) >= d.ticket:
                            continue
                        waited[sid] = d.ticket
                        eng.wait_ge(d.sem, d.ticket)
                    if op.fn is None:
                        continue
                    ins = op.fn(eng)
                    if op.key is not None:
                        ins.then_inc(op.sem, 16)
                    elif op.signal:
                        ins.then_inc(op.sem, 1)
            return body

        for e in self.ENGS:
            if self.ops[e]:
                getattr(block, handles[e])(make(e))


class Tl:
    __slots__ = ("ap", "res")

    def __init__(self, ap, name):
        self.ap = ap
        self.res = Res(name)

    def __getitem__(self, k):
        return self.ap[k]


def build(dbg=False):
    nc = bass.Bass("TRN2", target_bir_lowering=False)
    P = Prog()

    def din(name, shape, dt=F32):
        return nc.dram_tensor(name, list(shape), dt, kind="ExternalInput").ap()

    def dscr(name, shape, dt):
        kind = "ExternalOutput" if (dbg and name in ("CATT", "QF", "KHB", "DBG")) else "Internal"
        return Tl(nc.dram_tensor(name, list(shape), dt, kind=kind).ap(), name)

    x_own = din("x_own", [NT, D])
    x_pre = din("x_pre", [NPRE, D])
    x_ctx = din("x_ctx", [NCTX, D])
    w_in = din("w_in", [D, IN_COLS])
    w_o = din("w_o", [D, D])
    w_1 = din("w_1", [D, DFF])
    w_2 = din("w_2", [DFF, D])
    w_ada = din("w_ada", [D, 6 * D])
    craw_d = din("craw", [128, KC, 2])
    bada_d = din("bada", [1, 6 * D])
    pre1T_d = din("pre1T", [128, KC])
    pre2T_d = din("pre2T", [128, KC])
    post1_d = din("post1", [1, D])
    post2_d = din("post2", [1, D])
    glag_d = din("glag", [1, 2048])
    lng_d = din("lng", [1, 2048])
    lnb_d = din("lnb", [1, 2048])
    wdec_d = din("wdec", [33, 4, 512])
    wsT_d = din("wsT", [128, 4, 128])
    bs_d = din("bs", [1, 4, 512])
    cs_own_d = din("cs_own", [128, 2, NT])
    cs_pre_d = din("cs_pre", [128, 2, NPRE])
    consts_d = din("consts", [128, 6, 128])
    y_out = Tl(nc.dram_tensor("y", [NT, D], F32, kind="ExternalOutput").ap(), "y")

    QD = [[dscr(f"{n}{d}", [8, 128, NT], BF16) for d in "FB"] for n in "QK"]
    KH = [dscr(f"KH{d}", [8, 128, NCH, 128], BF16) for d in "FB"]
    VD = dscr("VD", [NT, 2048], BF16)
    RD = dscr("RD", [NT, 2048], BF16)
    CATT = dscr("CATT", [D, NT], BF16)
    X1 = dscr("X1", [NT, D], F32)
    MODROW = dscr("MODROW", [2, 6 * D], F32)
    DBG = dscr("DBG", [128, 4096], F32)
    WSCRS = [dscr(f"WSCR{i}", [72, 128, 8192], BF16) for i in range(2)]

    st = ExitStack()
    with st:
        ARENA_B = 200 * 1024
        arena = st.enter_context(nc.sbuf_tensor("arena", [128, ARENA_B // 2], BF16))
        state = {"off": 0}

        def mark():
            return state["off"]

        def release(m):
            state["off"] = m

        def sb(name, shape, dt):
            esz = 4 if dt == F32 else 2
            n = int(np.prod(shape[1:]))
            nb = (n * esz + 63) // 64 * 64
            off = state["off"]
            assert off + nb <= ARENA_B, (name, off, nb)
            state["off"] = off + nb
            state["max"] = max(state.get("max", 0), off + nb)
            if os.environ.get("KDBG_ALLOC"):
                print("ALLOC", name, off, nb, flush=True)
            offs[name] = off
            v = arena[0:shape[0], off // 2:(off + n * esz) // 2]
            if dt == F32:
                v = v.bitcast(F32)
            if len(shape) == 3:
                v = v.rearrange("p (a b) -> p a b", a=shape[1])
            elif len(shape) == 4:
                v = v.rearrange("p (a b c) -> p a b c", a=shape[1], b=shape[2])
            return Tl(v, name)

        offs = {}

        def arena_view(tile, nbytes, dt):
            off = offs[tile.res.name]
            v = arena[:, off // 2:(off + nbytes) // 2]
            return v.bitcast(F32) if dt == F32 else v

        banks = [st.enter_context(nc.psum_tensor(f"pb{i}", [128, 512], F32)) for i in range(8)]
        PB = [Tl(b[:], f"pb{i}") for i, b in enumerate(banks)]
        PBH = [b[:].bitcast(BF16) for b in banks]

        bank_rr = {"i": 0}

        def nextbank(lo=0, hi=6):
            i = bank_rr["i"]
            bank_rr["i"] = (i + 1 - lo) % (hi - lo) + lo if lo <= i < hi else lo
            return i if lo <= i < hi else lo

        dma_rr = {"i": 0}

        def dma(out_t, out_ap, in_t, in_ap, eng="sp", key=None, part=True):
            k = key if key is not None else out_t.res.name
            if k == "spill":
                k = "sp_" + in_t.res.name
            P.add(eng, lambda e: e.dma_start(out=out_ap, in_=in_ap),
                  reads=[in_t.res] if in_t is not None else [],
                  parts=[out_t.res] if part else [], writes=[] if part else [out_t.res], key=k)

        CONSTS = sb("consts", [128, 6, 128], F32)
        dma(CONSTS, CONSTS.ap, None, consts_d)
        IDF = CONSTS[:, 0, :]
        ROPEP = CONSTS[:, 1, :]
        MK = [CONSTS[:, 2 + i, :] for i in range(4)]
        IDB = sb("idb", [128, 128], BF16)
        P.add("dve", lambda e: e.tensor_copy(out=IDB.ap, in_=IDF), reads=[CONSTS.res], writes=[IDB.res])
        CV = sb("cvals", [128, 4], F32)
        P.add("dve", lambda e: e.memset(CV[:, 0:1], 1.0), parts=[CV.res])
        P.add("dve", lambda e: e.memset(CV[:, 1:2], EPS), parts=[CV.res])
        P.add("dve", lambda e: e.memset(CV[:, 2:3], 0.0), parts=[CV.res])
        ONE = CV[:, 0:1]
        EPSC = CV[:, 1:2]
        ONESF = sb("onesf", [128, 128], F32)
        P.add("dve", lambda e: e.memset(ONESF.ap, 1.0), writes=[ONESF.res])
        ONESB = sb("onesb", [128, 128], BF16)
        P.add("dve", lambda e: e.memset(ONESB.ap, 1.0), writes=[ONESB.res])
        MODT = sb("modT", [128, 192], F32)
        MODCT = sb("modcT", [128, 64], F32)
        A1 = sb("A1", [128, KC], F32)
        A1C = sb("A1C", [128, KC], F32)
        A2 = sb("A2", [128, KC], F32)
        PRE1T = sb("pre1T", [128, KC], F32)
        PRE2T = sb("pre2T", [128, KC], F32)
        dma(PRE1T, PRE1T.ap, None, pre1T_d)
        dma(PRE2T, PRE2T.ap, None, pre2T_d)
        base3_mark = mark()
        DEC = sb("DEC", [128, 2, 8, NCH], F32)
        SST = [sb(f"S{d}", [128, 8, 256], F32) for d in range(2)]
        WDEC = sb("wdecs", [33, 4, 512], F32)
        dma(WDEC, WDEC.ap, None, wdec_d)
        WLR = sb("wlr", [128, KC, 32], BF16)
        dma(WLR, WLR.ap, None, w_in[:, LF0:LF0 + 32].rearrange("(c p) n -> p c n", p=128), eng="pool")
        WST = sb("wst", [128, 4, 128], BF16)
        dma(WST, WST.ap, None, wsT_d, eng="pool")
        BS2 = sb("bs2", [2, 4, 512], BF16)

        def act(out_ap, in_ap, func, reads, writes, scale=1.0, bias=None, accum=None, parts=()):
            kw = {}
            if bias is not None:
                kw["bias"] = bias
            if accum is not None:
                kw["accum_out"] = accum
            P.add("act", lambda e: e.activation(out=out_ap, in_=in_ap, func=func, scale=scale, **kw),
                  reads=reads, writes=writes, parts=parts)

        mtmp = mark()
        BSF = sb("bsf", [1, 4, 512], F32)
        BSL = sb("bsl", [1, 4, 512], F32)
        BSLB = sb("bslb", [1, 4, 512], BF16)
        dma(BSF, BSF.ap, None, bs_d)
        P.add("dve", lambda e: e.tensor_copy(out=BS2[0:1, :, :], in_=BSF.ap), reads=[BSF.res], parts=[BS2.res])
        P.add("dve", lambda e: e.tensor_copy(out=BSL.ap, in_=BS2[0:1, :, :]), reads=[BS2.res], writes=[BSL.res])
        P.add("dve", lambda e: e.tensor_tensor(out=BSL.ap, in0=BSF.ap, in1=BSL.ap, op=ALU.subtract),
              reads=[BSF.res, BSL.res], writes=[BSL.res])
        P.add("dve", lambda e: e.tensor_copy(out=BSLB.ap, in_=BSL.ap), reads=[BSL.res], writes=[BSLB.res])
        dma(BS2, BS2[1:2, :, :], BSLB, BSLB.ap)
        P.barrier()
        release(mtmp)

        CRAW = sb("craw", [128, KC, 2], F32)
        dma(CRAW, CRAW.ap, None, craw_d)
        act(CRAW.ap, CRAW.ap, AF.Silu, [CRAW.res], [CRAW.res])
        base_mark = mark()

        class Ada:
            def __init__(self, nwa=3):
                self.nwa = nwa
                self.CONDB = sb("condB", [128, KC, 128], F32)
                self.WA = [sb(f"wa{i}", [128, 8, 512], F32) for i in range(nwa)]
                self.BROW = sb("brow", [128, 512], F32)
                self.MB = sb("mb", [128, 512], F32)
                self.wq = 0
                CONDB = self.CONDB
                for kc in range(KC):
                    P.add("dve", lambda e, kc=kc: e.tensor_scalar(out=CONDB[:, kc, 0:64], in0=ONESF[:, 0:64], scalar1=CRAW[:, kc, 0:1],
                                                                  scalar2=None, op0=ALU.mult), reads=[CRAW.res, ONESF.res], parts=[CONDB.res])
                    P.add("dve", lambda e, kc=kc: e.tensor_scalar(out=CONDB[:, kc, 64:128], in0=ONESF[:, 0:64], scalar1=CRAW[:, kc, 1:2],
                                                                  scalar2=None, op0=ALU.mult), reads=[CRAW.res, ONESF.res], parts=[CONDB.res])

            def steps(self, cbs, banks):
                CONDB, BROW, MB = self.CONDB, self.BROW, self.MB
                for n, cb in enumerate(cbs):
                    bk = PB[banks[0] + (n % 2 if banks[2] else 0)]
                    for q4 in range(4):
                        wa = self.WA[self.wq % self.nwa]
                        self.wq += 1
                        dma(wa, wa.ap, None, w_ada[q4 * 1024:(q4 + 1) * 1024, cb * 512:(cb + 1) * 512].rearrange("(c p) n -> p c n", p=128), part=False)
                        for k8 in range(8):
                            kc = q4 * 8 + k8
                            P.add("pe", lambda e, bk=bk, wa=wa, k8=k8, kc=kc: e.matmul(bk.ap, lhsT=CONDB[:, kc, :], rhs=wa[:, k8, :],
                                                                                       start=(kc == 0), stop=(kc == KC - 1)),
                                  reads=[CONDB.res, wa.res], parts=[bk.res])
                        if q4 < 3:
                            yield
                    dma(BROW, BROW.ap, None, bada_d[0:1, cb * 512:(cb + 1) * 512].partition_broadcast(128), part=False)
                    P.add("dve", lambda e, bk=bk: e.tensor_tensor(out=MB.ap, in0=bk.ap, in1=BROW.ap, op=ALU.add),
                          reads=[BROW.res], writes=[MB.res, bk.res])
                    dma(MODROW, MODROW[0:1, cb * 512:(cb + 1) * 512], MB, MB[0:1, :], key="modrow")
                    tb = PB[banks[1] + (n % 2 if banks[2] else 0)]
                    for j in range(4):
                        P.add("pe", lambda e, tb=tb, j=j: e.transpose(tb[:, j * 128:(j + 1) * 128], MB[:, j * 128:(j + 1) * 128], IDF),
                              reads=[MB.res, CONSTS.res], parts=[tb.res])
                    tv = tb.ap.rearrange("p (j c) -> p j c", j=4)
                    P.add("dve", lambda e, tv=tv, cb=cb: e.tensor_copy(out=MODT[:, cb * 4:(cb + 1) * 4], in_=tv[:, :, 0]),
                          parts=[MODT.res], writes=[tb.res])
                    if cb < 16:
                        P.add("dve", lambda e, tv=tv, cb=cb: e.tensor_copy(out=MODCT[:, cb * 4:(cb + 1) * 4], in_=tv[:, :, 64]),
                              parts=[MODCT.res], writes=[tb.res])
                    yield

        m0 = mark()
        ada0 = Ada()
        for _ in ada0.steps(range(48), (0, 2, True)):
            pass
        for (dst, mod, off, pre) in ((A1, MODT, 32, PRE1T), (A1C, MODCT, 32, PRE1T), (A2, MODT, 128, PRE2T)):
            P.add("dve", lambda e, dst=dst, mod=mod, off=off, pre=pre: e.scalar_tensor_tensor(
                out=dst.ap, in0=mod[:, off:off + KC], scalar=1.0, in1=pre.ap, op0=ALU.add, op1=ALU.mult),
                reads=[mod.res, pre.res], writes=[dst.res])
        SH1 = MODT[:, 0:KC]
        SH1C = MODCT[:, 0:KC]
        SH2 = MODT[:, 96:128]
        P.barrier()
        release(m0)

        m1 = mark()
        HXT = sb("hxT", [128, KC, T], BF16)
        WB = [sb(f"wb{i}", [128, 8192], BF16) for i in range(NWB)]
        XB = sb("xblk", [128, D], F32)
        XN = sb("xn", [128, D], BF16)
        SSs = [sb(f"ss{i}", [128, 2], F32) for i in range(4)]
        LRT = sb("lrT", [33, T], F32)
        P.add("dve", lambda e: e.memset(LRT[32:33, :], 1.0), parts=[LRT.res])
        CS = sb("cs", [128, 2, T], F32)
        QRAW = sb("qraw", [128, T], F32)
        RT1 = sb("rt1", [128, T], F32)
        QKR = sb("qkr", [128, 4, T], F32)
        STG = sb("stg", [128, 8, T], BF16)
        SP = sb("sp", [128, 512], F32)
        EQ = sb("eq", [128, 512], F32)
        EK = sb("ek", [128, 512], F32)
        EKH = sb("ekh", [128, 512], F32)
        KTOK = sb("ktok", [128, 256], F32)
        KHS = sb("khs", [128, 512], BF16)
        VV = sb("vv", [128, 4, 2048], BF16)
        LNG = sb("lngb", [128, 2048], BF16)
        LNB = sb("lnbb", [128, 2048], BF16)
        VST = sb("vst", [128, 4, 512], BF16)
        UT = sb("ut", [128, 2, T], BF16)
        YSG = sb("ysg", [128, 2, T], BF16)
        XBs = [(XB.ap, [XB.res]), (arena_view(VV, 16384, F32), [VV.res])]
        XNs = [(XN.ap, [XN.res]), (arena_view(VST, 8192, BF16), [VST.res, UT.res, YSG.res])]
        LNS = sb("lns", [128, 16], F32)
        DECT = sb("dect", [128, 4], F32)
        LNTMP = Tl(XB[:, 0:2048], "lntmp")
        LNTMP.res = XB.res
        dma(LNTMP, LNTMP.ap, None, lng_d.partition_broadcast(128), part=False)
        P.add("dve", lambda e: e.tensor_copy(out=LNG.ap, in_=LNTMP.ap), reads=[LNTMP.res], writes=[LNG.res])
        dma(LNTMP, LNTMP.ap, None, lnb_d.partition_broadcast(128), part=False)
        P.add("dve", lambda e: e.tensor_copy(out=LNB.ap, in_=LNTMP.ap), reads=[LNTMP.res], writes=[LNB.res])
        for d in range(2):
            P.add("pool", lambda e, d=d: e.memset(SST[d].ap, 0.0), writes=[SST[d].res])

        wstate = {"n": 0}

        wscr = {"mode": None, "blk": 0, "nprec": 0}

        def load_w(src_ap_list, shape3):
            wb = WB[wstate["n"] % NWB]
            wstate["n"] += 1
            view = wb.ap.rearrange("p (a b) -> p a b", a=shape3[0])
            if wscr["mode"] == "rd" or (wscr["mode"] == "wb" and wscr["blk"] < wscr["nprec"]):
                blk = wscr["blk"]
                wscr["blk"] += 1
                for hf in range(2):
                    o = wb.ap[:, hf * 4096:(hf + 1) * 4096]
                    WSCR = WSCRS[blk // 72]
                    src = WSCR[blk % 72, :, hf * 4096:(hf + 1) * 4096]
                    P.add("sp", lambda e, o=o, src=src: e.dma_start(out=o, in_=src), reads=[WSCR.res],
                          writes=[wb.res] if hf == 0 else [], parts=[wb.res] if hf == 1 else [], key=wb.res.name + '_h')
                return wb, view
            first = True
            for (dst_sl, src) in src_ap_list:
                o = view[:, dst_sl[0], dst_sl[1]]
                P.add("pool", lambda e, o=o, src=src: e.dma_start(out=o, in_=src),
                      writes=[wb.res] if first else [], parts=[] if first else [wb.res], key=wb.res.name)
                first = False
            if wscr["mode"] == "wb":
                blk = wscr["blk"]
                wscr["blk"] += 1
                WSCR = WSCRS[blk // 72]
                dst = WSCR[blk % 72, :, :]
                P.add("sp", lambda e, dst=dst: e.dma_start(out=dst, in_=wb.ap), reads=[wb.res], parts=[WSCR.res], key="wbk_" + wb.res.name)
            return wb, view

        def wsrc(w, r0, r1, c0, c1):
            return w[r0:r1, c0:c1].rearrange("(c p) n -> p c n", p=128)

        def load_fm(w, c0, ncols=256):
            return load_w([((slice(0, 16), slice(0, ncols)), wsrc(w, 0, 2048, c0, c0 + ncols)),
                           ((slice(16, 32), slice(0, ncols)), wsrc(w, 2048, 4096, c0, c0 + ncols))], (32, 256))

        def load_tm(w, r0, c0):
            return load_w([((slice(0, 8), slice(0, 512)), wsrc(w, r0, r0 + 1024, c0, c0 + 512)),
                           ((slice(8, 16), slice(0, 512)), wsrc(w, r0 + 1024, r0 + 2048, c0, c0 + 512))], (16, 512))

        def norm_mod_T(x_dram_rows, ntb, A, SH, dstT):
            modres = [MODT.res, MODCT.res, A1.res, A1C.res, A2.res]

            def front(tb):
                xb, xbr = XBs[tb % 2]
                xn, xnr = XNs[tb % 2]
                ss = SSs[tb]
                P.add("sp", lambda e: e.dma_start(out=xb, in_=x_dram_rows(tb)), writes=xbr, key="xb%d" % (tb % 2))
                P.add("dve", lambda e: e.memset(ss[:, 0:1], 0.0), writes=[ss.res])
                act(JUNK, xb, AF.Square, xbr + [ss.res], [STG.res], accum=ss[:, 0:1], parts=[ss.res])
                act(ss[:, 1:2], ss[:, 0:1], AF.Ln, [ss.res, CV.res], [], scale=1.0 / D, bias=EPSC, parts=[ss.res])
                act(ss[:, 1:2], ss[:, 1:2], AF.Exp, [ss.res], [], scale=-0.5, parts=[ss.res])
                P.add("dve", lambda e: e.tensor_scalar(out=xn, in0=xb, scalar1=ss[:, 1:2], scalar2=None, op0=ALU.mult),
                      reads=xbr + [ss.res], writes=xnr)

            def back(tb):
                xn, xnr = XNs[tb % 2]
                for k2 in range(KC // 8):
                    bi = 6 + (k2 % 2)
                    for j in range(8):
                        kc = k2 * 8 + j
                        P.add("pe", lambda e, bi=bi, j=j, kc=kc: e.transpose(PBH[bi][:, j * 128:(j + 1) * 128], xn[:, kc * 128:(kc + 1) * 128], IDB.ap),
                              reads=xnr + [IDB.res], parts=[PB[bi].res])
                    for j in range(8):
                        kc = k2 * 8 + j
                        if k2 % 2 == 0 or os.environ.get('KNOACT'):
                            P.add("dve", lambda e, bi=bi, j=j, kc=kc: e.tensor_scalar(
                                out=dstT[:, kc, tb * 128:(tb + 1) * 128], in0=PBH[bi][:, j * 128:(j + 1) * 128], scalar1=A[:, kc:kc + 1], scalar2=SH[:, kc:kc + 1],
                                op0=ALU.mult, op1=ALU.add), reads=modres, writes=[PB[bi].res], parts=[HXT.res])
                        else:
                            act(dstT[:, kc, tb * 128:(tb + 1) * 128], PBH[bi][:, j * 128:(j + 1) * 128], AF.Identity, modres, [PB[bi].res],
                                scale=A[:, kc:kc + 1], bias=SH[:, kc:kc + 1], parts=[HXT.res])

            JUNK = arena_view(STG, 8192, BF16)
            front(0)
            if ntb > 1:
                front(1)
            for tb in range(ntb):
                back(tb)
                if tb + 2 < ntb:
                    front(tb + 2)

        def mm_fm(view, j, bank, ntok, wb, hx=None):
            hx = HXT if hx is None else hx
            for kc in range(KC):
                P.add("pe", lambda e, kc=kc: e.matmul(bank[:, 0:ntok], lhsT=view[:, kc, j * 128:(j + 1) * 128], rhs=hx[:, kc, 0:ntok],
                                                      start=(kc == 0), stop=(kc == KC - 1)),
                      reads=[hx.res, wb.res], parts=[bank.res])

        def decay_and_state(pair, c, tile_tok0, rope, own, dirs, gchunk):
            c0 = c * 128
            ab = PB[2]
            P.add("pe", lambda e: e.matmul(ab.ap, lhsT=LRT[:, c0:c0 + 128], rhs=WDEC[:, pair, :], start=True, stop=True),
                  reads=[LRT.res, WDEC.res], parts=[ab.res])
            act(SP.ap, ab.ap, AF.Exp, [], [SP.res, ab.res], scale=-1.0)
            act(SP.ap, SP.ap, AF.Ln, [SP.res, CV.res], [SP.res], bias=ONE)
            bb = PB[3]
            for d in range(2):
                for hh in range(2):
                    blk = d * 2 + hh
                    P.add("pe", lambda e, d=d, blk=blk: e.matmul(bb[:, blk * 128:(blk + 1) * 128], lhsT=SP[:, blk * 128:(blk + 1) * 128], rhs=MK[d],
                                                                   start=(blk == 0), stop=(blk == 3), skip_group_check=True),
                          reads=[SP.res, CONSTS.res], parts=[bb.res])
            act(EQ.ap, bb.ap, AF.Exp, [], [EQ.res, bb.res], scale=-1.0 / 16)
            if own:
                act(EK.ap, bb.ap, AF.Exp, [], [EK.res, bb.res], scale=1.0 / 16)
            xb_ = PB[4]
            for d in range(2):
                P.add("pe", lambda e, d=d: e.matmul(xb_[:, d * 256:(d + 1) * 256], lhsT=MK[2 + d], rhs=SP[:, d * 256:(d + 1) * 256],
                                                    start=(d == 0), stop=(d == 1), skip_group_check=True),
                      reads=[SP.res, CONSTS.res], parts=[xb_.res])
            act(EKH.ap, xb_.ap, AF.Exp, [], [EKH.res, xb_.res], scale=-1.0 / 16)
            kt = PB[5]
            for hh in range(2):
                P.add("pe", lambda e, hh=hh: e.transpose(kt[:, hh * 128:(hh + 1) * 128], QKR[:, 2 + hh, c0:c0 + 128], IDF),
                      reads=[QKR.res, CONSTS.res], parts=[kt.res])
            P.add("dve", lambda e: e.tensor_copy(out=KTOK.ap, in_=kt[:, 0:256]), writes=[KTOK.res, kt.res])
            for d in range(2):
                P.add("dve", lambda e, d=d: e.tensor_tensor(out=KHS[:, d * 256:(d + 1) * 256], in0=KTOK.ap, in1=EKH[:, d * 256:(d + 1) * 256], op=ALU.mult),
                      reads=[KTOK.res, EKH.res], parts=[KHS.res])
            if own:
                for d in range(2):
                    last = 127 if d == 0 else 0
                    ev = EQ.ap.rearrange("p (b i) -> p b i", b=4)
                    P.add("pool", lambda e, d=d, ev=ev, last=last: e.tensor_copy(out=DEC[:, d, 2 * pair:2 * pair + 2, gchunk], in_=ev[:, 2 * d:2 * d + 2, last]),
                          reads=[EQ.res], parts=[DEC.res])
                    P.add("dve", lambda e, d=d, ev=ev: e.tensor_tensor(out=STG[:, 2 * d:2 * d + 2, c0:c0 + 128], in0=QKR[:, 0:2, c0:c0 + 128],
                                                                       in1=ev[:, 2 * d:2 * d + 2, :], op=ALU.mult),
                          reads=[QKR.res, EQ.res], parts=[STG.res])
                    ekv = EK.ap.rearrange("p (b i) -> p b i", b=4)
                    P.add("dve", lambda e, d=d, ekv=ekv: e.tensor_tensor(out=STG[:, 4 + 2 * d:6 + 2 * d, c0:c0 + 128], in0=QKR[:, 2:4, c0:c0 + 128],
                                                                         in1=ekv[:, 2 * d:2 * d + 2, :], op=ALU.mult),
                          reads=[QKR.res, EK.res], parts=[STG.res])
                    dma(KH[d], KH[d][2 * pair:2 * pair + 2, :, gchunk, :].rearrange("h p c -> p h c"), KHS,
                        KHS[:, d * 256:(d + 1) * 256].rearrange("p (h c) -> p h c", h=2), key="spill")
            return EQ, KHS

        def phase1_tile(kind, ti):
            ntok = NCTX if kind == "ctx" else T
            ntb = ntok // 128
            own = kind == "own"
            if kind == "own":
                xr = lambda tb: x_own[ti * T + tb * 128: ti * T + (tb + 1) * 128, :]
                A, SH = A1, SH1
            elif kind == "pre":
                xr = lambda tb: x_pre[ti * T + tb * 128: ti * T + (tb + 1) * 128, :]
                A, SH = A1, SH1
            else:
                xr = lambda tb: x_ctx[tb * 128:(tb + 1) * 128, :]
                A, SH = A1C, SH1C
            norm_mod_T(xr, ntb, A.ap, SH, HXT.ap)
            rope = kind != "ctx"
            if rope:
                src = cs_own_d if own else cs_pre_d
                dma(CS, CS.ap, None, src[:, :, ti * T:(ti + 1) * T], part=False)
            lb = PB[0]
            for kc in range(KC):
                P.add("pe", lambda e, kc=kc: e.matmul(lb[0:32, 0:ntok], lhsT=WLR[:, kc, :], rhs=HXT[:, kc, 0:ntok], start=(kc == 0), stop=(kc == KC - 1)),
                      reads=[HXT.res, WLR.res], parts=[lb.res])
            P.add("dve", lambda e: e.tensor_copy(out=LRT[0:32, 0:ntok], in_=lb[0:32, 0:ntok]), writes=[LRT.res, lb.res])
            if not own:
                VT = VV
                for vb in range(4):
                    for half in range(2):
                        wb, view = load_tm(w_in, half * 2048, V0 + vb * 512)
                        for tb in range(ntb):
                            bank = PB[tb % 2] if False else PB[tb]
                            for k16 in range(16):
                                kc = half * 16 + k16
                                P.add("pe", lambda e, bank=bank, view=view, tb=tb, kc=kc, k16=k16: e.matmul(
                                    bank.ap, lhsT=HXT[:, kc, tb * 128:(tb + 1) * 128], rhs=view[:, k16, :], start=(kc == 0), stop=(kc == KC - 1),
                                    skip_group_check=True), reads=[HXT.res, wb.res], parts=[bank.res])
                    for tb in range(ntb):
                        bank = PB[tb]
                        act(VT[:, tb, vb * 512:(vb + 1) * 512], bank.ap, AF.Copy, [], [bank.res], parts=[VT.res])
            if not own:
                dirs = (0, 1) if kind == "ctx" else (1,)
                wsp = 512 if kind == "ctx" else 256
                coff = 0 if kind == "ctx" else 256
                SP2 = [arena_view(STG, 8192, F32)[:, par * 1024:(par + 1) * 1024].rearrange("p (c w) -> p c w", c=ntb) for par in range(2)]
                KTOK4 = arena_view(SP, 4096, F32).rearrange("p (c w) -> p c w", c=4)
                KHS4 = arena_view(EK, 2048, BF16).rearrange("p (c w) -> p c w", c=4)

                def dsl(d):
                    return slice(d * 256, (d + 1) * 256) if kind == "ctx" else slice(0, 256)

                def stageA(p):
                    par = p % 2
                    for c in range(ntb):
                        bank = PB[4 + c % 2]
                        P.add("pe", lambda e, bank=bank, c=c: e.matmul(bank[:, 0:wsp], lhsT=LRT[:, c * 128:(c + 1) * 128], rhs=WDEC[:, p, coff:coff + wsp],
                                                                      start=True, stop=True), reads=[LRT.res, WDEC.res], parts=[bank.res])
                        act(SP2[par][:, c, :], bank[:, 0:wsp], AF.Exp, [], [bank.res], scale=-1.0, parts=[STG.res])
                    act(SP2[par], SP2[par], AF.Ln, [STG.res, CV.res], [], bias=ONE, parts=[STG.res])

                def kblock(p):
                    wb2, view2 = load_fm(w_in, K0 + p * 256)
                    for hh in range(2):
                        qi = 2 * (p % 2) + hh
                        bank = PB[hh]
                        mm_fm(view2, hh, bank, ntok, wb2)
                        if rope:
                            act(QRAW[:, 0:ntok], bank[:, 0:ntok], AF.Copy, [], [QRAW.res, bank.res])
                            rp = PB[4 + hh]
                            P.add("pe", lambda e, rp=rp: e.matmul(rp[:, 0:ntok], lhsT=ROPEP, rhs=QRAW[:, 0:ntok], start=True, stop=True),
                                  reads=[QRAW.res, CONSTS.res], parts=[rp.res])
                            P.add("dve", lambda e: e.tensor_tensor(out=RT1[:, 0:ntok], in0=QRAW[:, 0:ntok], in1=CS[:, 0, 0:ntok], op=ALU.mult),
                                  reads=[QRAW.res, CS.res], writes=[RT1.res])
                            P.add("dve", lambda e, rp=rp, qi=qi: e.tensor_tensor(out=QKR[:, qi, 0:ntok], in0=rp[:, 0:ntok], in1=CS[:, 1, 0:ntok], op=ALU.mult),
                                  reads=[CS.res], parts=[QKR.res], writes=[rp.res])
                            P.add("dve", lambda e, qi=qi: e.tensor_tensor(out=QKR[:, qi, 0:ntok], in0=QKR[:, qi, 0:ntok], in1=RT1[:, 0:ntok], op=ALU.add),
                                  reads=[QKR.res, RT1.res], parts=[QKR.res])
                        else:
                            act(QKR[:, qi, 0:ntok], bank[:, 0:ntok], AF.Copy, [], [bank.res], parts=[QKR.res])

                def stageB(p):
                    par = p % 2
                    kt = PB[2]
                    for c in range(ntb):
                        for hh in range(2):
                            qi = 2 * par + hh
                            P.add("pe", lambda e, hh=hh, qi=qi, c=c: e.transpose(kt[:, hh * 128:(hh + 1) * 128], QKR[:, qi, c * 128:(c + 1) * 128], IDF),
                                  reads=[QKR.res, CONSTS.res], parts=[kt.res])
                        P.add("dve", lambda e, c=c: e.tensor_copy(out=KTOK4[:, c, :], in_=kt[:, 0:256]), writes=[kt.res], parts=[SP.res, EQ.res])
                    for d in dirs:
                        for jb in range(ntb):
                            xb_ = PB[3]
                            after = [mb for mb in range(ntb) if (mb > jb if d == 0 else mb < jb)]
                            P.add("pe", lambda e, d=d, jb=jb, after=after: e.matmul(xb_[:, 0:256], lhsT=MK[2 + d], rhs=SP2[par][:, jb, dsl(d)],
                                                                                  start=True, stop=(not after)),
                                  reads=[STG.res, CONSTS.res], parts=[xb_.res])
                            for i_, mb in enumerate(after):
                                P.add("pe", lambda e, d=d, mb=mb, i_=i_, after=after: e.matmul(xb_[:, 0:256], lhsT=ONESF.ap, rhs=SP2[par][:, mb, dsl(d)],
                                                                                              start=False, stop=(i_ == len(after) - 1)),
                                      reads=[STG.res, ONESF.res], parts=[xb_.res])
                            act(EKH[:, 0:256], xb_[:, 0:256], AF.Exp, [], [EKH.res, xb_.res], scale=-1.0 / 16)
                            P.add("dve", lambda e, jb=jb: e.tensor_tensor(out=KHS4[:, jb, :], in0=KTOK4[:, jb, :], in1=EKH[:, 0:256], op=ALU.mult),
                                  reads=[SP.res, EQ.res, EKH.res], parts=[EK.res])
                        db = PB[2]
                        for hh in range(2):
                            for c in range(ntb):
                                sl = slice(dsl(d).start + hh * 128, dsl(d).start + (hh + 1) * 128)
                                P.add("pe", lambda e, hh=hh, c=c, sl=sl: e.matmul(db[:, 256 + hh:257 + hh], lhsT=SP2[par][:, c, sl], rhs=ONESF[:, 0:1],
                                                                                 start=(c == 0), stop=(c == ntb - 1), skip_group_check=True),
                                      reads=[STG.res, ONESF.res], parts=[db.res])
                        act(DECT[:, 0:2], db[:, 256:258], AF.Exp, [], [DECT.res, db.res], scale=-1.0 / 16)
                        for hh in range(2):
                            h = 2 * p + hh
                            ub = PB[6 + hh]
                            for c in range(ntb):
                                P.add("pe", lambda e, ub=ub, hh=hh, c=c, h=h: e.matmul(
                                    ub[:, 0:256], lhsT=KHS4[:, c, hh * 128:(hh + 1) * 128], rhs=VV[:, c, h * 256:(h + 1) * 256],
                                    start=(c == 0), stop=(c == ntb - 1)), reads=[EK.res, VV.res], parts=[ub.res])
                            P.add("dve", lambda e, ub=ub, d=d, h=h, hh=hh: e.scalar_tensor_tensor(
                                out=SST[d][:, h, :], in0=SST[d][:, h, :], scalar=DECT[:, hh:hh + 1], in1=ub[:, 0:256],
                                op0=ALU.mult, op1=ALU.add), reads=[DECT.res, SST[d].res], writes=[ub.res], parts=[SST[d].res])

                stageA(0)
                kblock(0)
                for p in range(1, 4):
                    stageA(p)
                    kblock(p)
                    stageB(p - 1)
                stageB(3)
                return
            for pair in range(4):
                wb, view = load_fm(w_in, Q0 + pair * 256)
                blocks = [(wb, view, 0), (wb, view, 1)]
                wb2, view2 = load_fm(w_in, K0 + pair * 256)
                blocks += [(wb2, view2, 0), (wb2, view2, 1)]
                for (wbx, vw, j), qi in zip(blocks, [0, 1, 2, 3]):
                    bank = PB[qi % 2]
                    mm_fm(vw, j, bank, ntok, wbx)
                    scale = (128.0 ** -0.5) if qi < 2 else 1.0
                    act(QRAW[:, 0:ntok], bank[:, 0:ntok], AF.Copy, [], [QRAW.res, bank.res], scale=scale)
                    rp = PB[6 + qi % 2]
                    P.add("pe", lambda e, rp=rp: e.matmul(rp[:, 0:ntok], lhsT=ROPEP, rhs=QRAW[:, 0:ntok], start=True, stop=True),
                          reads=[QRAW.res, CONSTS.res], parts=[rp.res])
                    P.add("dve", lambda e: e.tensor_tensor(out=RT1[:, 0:ntok], in0=QRAW[:, 0:ntok], in1=CS[:, 0, 0:ntok], op=ALU.mult),
                          reads=[QRAW.res, CS.res], writes=[RT1.res])
                    P.add("dve", lambda e, rp=rp, qi=qi: e.tensor_tensor(out=QKR[:, qi, 0:ntok], in0=rp[:, 0:ntok], in1=CS[:, 1, 0:ntok], op=ALU.mult),
                          reads=[CS.res], parts=[QKR.res], writes=[rp.res])
                    P.add("dve", lambda e, qi=qi: e.tensor_tensor(out=QKR[:, qi, 0:ntok], in0=QKR[:, qi, 0:ntok], in1=RT1[:, 0:ntok], op=ALU.add),
                          reads=[QKR.res, RT1.res], parts=[QKR.res])
                for c in range(ntb):
                    decay_and_state(pair, c, ti * T, rope, True, (0, 1), ti * 4 + c)
                for qk in range(2):
                    for d in range(2):
                        dma(QD[qk][d], QD[qk][d][2 * pair:2 * pair + 2, :, ti * T:(ti + 1) * T].rearrange("h p t -> p h t"), STG,
                            STG[:, qk * 4 + 2 * d: qk * 4 + 2 * d + 2, :], key="spill")
            for (c0col, dst, fn) in ((V0, VD, AF.Copy), (R0, RD, AF.Silu)):
                for vb in range(4):
                    for half in range(2):
                        wb, view = load_tm(w_in, half * 2048, c0col + vb * 512)
                        for tb in range(4):
                            bank = PB[tb]
                            for k16 in range(16):
                                kc = half * 16 + k16
                                P.add("pe", lambda e, bank=bank, view=view, tb=tb, kc=kc, k16=k16: e.matmul(
                                    bank.ap, lhsT=HXT[:, kc, tb * 128:(tb + 1) * 128], rhs=view[:, k16, :], start=(kc == 0), stop=(kc == KC - 1),
                                    skip_group_check=True), reads=[HXT.res, wb.res], parts=[bank.res])
                    for tb in range(4):
                        bank = PB[tb]
                        act(VST[:, tb, :], bank.ap, fn, [], [bank.res], parts=[VST.res])
                    dma(dst, dst[ti * T:(ti + 1) * T, vb * 512:(vb + 1) * 512].rearrange("(t p) n -> p t n", p=128), VST, VST.ap, key="spill")
            P.add("dve", lambda e: e.memset(LNS.ap, 0.0), writes=[LNS.res])
            for vb in range(4):
                for half in range(2):
                    wb, view = load_tm(w_in, half * 2048, VV0 + vb * 512)
                    for tb in range(4):
                        bank = PB[tb]
                        for k16 in range(16):
                            kc = half * 16 + k16
                            P.add("pe", lambda e, bank=bank, view=view, tb=tb, kc=kc, k16=k16: e.matmul(
                                bank.ap, lhsT=HXT[:, kc, tb * 128:(tb + 1) * 128], rhs=view[:, k16, :], start=(kc == 0), stop=(kc == KC - 1),
                                skip_group_check=True), reads=[HXT.res, wb.res], parts=[bank.res])
                for tb in range(4):
                    bank = PB[tb]
                    act(VV[:, tb, vb * 512:(vb + 1) * 512], bank.ap, AF.Gelu, [], [bank.res], parts=[VV.res])
            for tb in range(4):
                P.add("dve", lambda e, tb=tb: e.tensor_reduce(out=LNS[:, tb:tb + 1], in_=VV[:, tb, :], axis=AX.X, op=ALU.add),
                      reads=[VV.res], parts=[LNS.res])
                act(LNTMP.ap, VV[:, tb, :], AF.Square, [VV.res, LNS.res], [LNTMP.res], accum=LNS[:, 4 + tb:5 + tb], parts=[LNS.res])
            P.add("dve", lambda e: e.tensor_scalar(out=LNS[:, 0:4], in0=LNS[:, 0:4], scalar1=1.0 / 2048, scalar2=None, op0=ALU.mult),
                  reads=[LNS.res], writes=[LNS.res])
            P.add("dve", lambda e: e.tensor_tensor(out=LNS[:, 8:12], in0=LNS[:, 0:4], in1=LNS[:, 0:4], op=ALU.mult), reads=[LNS.res], writes=[LNS.res])
            P.add("dve", lambda e: e.scalar_tensor_tensor(out=LNS[:, 4:8], in0=LNS[:, 4:8], scalar=1.0 / 2048, in1=LNS[:, 8:12], op0=ALU.mult, op1=ALU.subtract),
                  reads=[LNS.res], writes=[LNS.res])
            act(LNS[:, 4:8], LNS[:, 4:8], AF.Ln, [LNS.res, CV.res], [LNS.res], bias=EPSC)
            act(LNS[:, 4:8], LNS[:, 4:8], AF.Exp, [LNS.res], [LNS.res], scale=-0.5)
            P.add("dve", lambda e: e.scalar_tensor_tensor(out=LNS[:, 8:12], in0=LNS[:, 0:4], scalar=-1.0, in1=LNS[:, 4:8], op0=ALU.mult, op1=ALU.mult),
                  reads=[LNS.res], writes=[LNS.res])
            for tb in range(4):
                act(LNTMP.ap, VV[:, tb, :], AF.Identity, [VV.res, LNS.res], [LNTMP.res], scale=LNS[:, 4 + tb:5 + tb], bias=LNS[:, 8 + tb:9 + tb])
                P.add("dve", lambda e: e.tensor_tensor(out=LNTMP.ap, in0=LNTMP.ap, in1=LNG.ap, op=ALU.mult), reads=[LNTMP.res, LNG.res], writes=[LNTMP.res])
                P.add("dve", lambda e, tb=tb: e.tensor_tensor(out=VV[:, tb, :], in0=LNTMP.ap, in1=LNB.ap, op=ALU.add), reads=[LNTMP.res, LNB.res], parts=[VV.res])
            for ub_ in range(8):
                wb, view = load_fm(w_in, U0 + ub_ * 256)
                for j in range(2):
                    cc = ub_ * 2 + j
                    g = cc // 4
                    bank = PB[j]
                    mm_fm(view, j, bank, T, wb)
                    act(UT[:, j, :], bank.ap, AF.Gelu, [], [bank.res], parts=[UT.res])
                    sbk = PB[2 + j]
                    P.add("pe", lambda e, sbk=sbk, g=g: e.matmul(sbk.ap, lhsT=ONESB[0:2, :], rhs=BS2[0:2, g, :], start=True, stop=False, skip_group_check=True),
                          reads=[ONESB.res, BS2.res], parts=[sbk.res])
                    for tb in range(4):
                        P.add("pe", lambda e, sbk=sbk, tb=tb, cc=cc, g=g: e.matmul(sbk[:, tb * 128:(tb + 1) * 128], lhsT=VV[:, tb, cc * 128:(cc + 1) * 128],
                                                                                 rhs=WST[:, g, :], start=False, stop=(tb == 3), skip_group_check=True),
                              reads=[VV.res, WST.res], parts=[sbk.res])
                    P.add("dve", lambda e, sbk=sbk, j=j: e.tensor_tensor(out=YSG[:, j, :], in0=sbk.ap, in1=UT[:, j, :], op=ALU.mult),
                          reads=[UT.res], writes=[sbk.res], parts=[YSG.res])
                dma(CATT, CATT[2048 + ub_ * 256: 2048 + (ub_ + 1) * 256, ti * T:(ti + 1) * T].rearrange("(j p) t -> p j t", p=128), YSG, YSG.ap, key="spill")

        phase1_tile("ctx", 0)
        for ti in range(NPRE // T - 1, -1, -1):
            phase1_tile("pre", ti)
        for ti in range(NT // T):
            phase1_tile("own", ti)
        P.barrier()
        release(m1)

        m2 = mark()
        GLAG = sb("glagb", [128, 2048], F32)
        dma(GLAG, GLAG.ap, None, glag_d.partition_broadcast(128), part=False)
        HQ2 = [[[sb(f"hq{b}{qk}{d}", [128, NT], BF16) for d in range(2)] for qk in range(2)] for b in range(2)]
        HKH2 = [[sb(f"hkh{b}{d}", [128, NCH, 128], BF16) for d in range(2)] for b in range(2)]
        HV2 = [sb(f"hv{b}", [128, NCH, 256], BF16) for b in range(2)]
        HR2 = [sb(f"hr{b}", [128, NCH, 256], BF16) for b in range(2)]

        def load_head(h):
            HQ, HKH, HV, HR = HQ2[h % 2], HKH2[h % 2], HV2[h % 2], HR2[h % 2]
            for qk in range(2):
                for d in range(2):
                    dma(HQ[qk][d], HQ[qk][d].ap, QD[qk][d], QD[qk][d][h, :, :], part=False)
            for d in range(2):
                dma(HKH[d], HKH[d].ap, KH[d], KH[d][h, :, :, :], part=False)
            dma(HV, HV.ap, VD, VD[:, h * 256:(h + 1) * 256].rearrange("(c p) n -> p c n", p=128), part=False)
            dma(HR, HR.ap, RD, RD[:, h * 256:(h + 1) * 256].rearrange("(c p) n -> p c n", p=128), part=False)

        OST = sb("ost", [128, NCH, 256], F32)
        YCT = sb("yct", [128, 2, NT], BF16)
        ATT = [sb(f"att{d}", [128, 128], BF16) for d in range(2)]
        OSUM = [sb(f"osum{d}", [128, 256], F32) for d in range(2)]
        YT = [sb(f"yt{d}", [128, 256], BF16) for d in range(2)]
        JNK = [sb(f"jnk{d}", [128, 256], BF16) for d in range(2)]
        RS = [sb(f"rs{d}", [128, 4], F32) for d in range(2)]
        def ada_pump(n):
            return

        SBA = [sb(f"sba{d}", [128, NCH, 256], BF16) for d in range(2)]
        def head_body(h, HQ, HKH, HV, HR):
                for s_ in range(NCH):
                    for d in range(2):
                        c = s_ if d == 0 else NCH - 1 - s_
                        act(SBA[d][:, c, :], SST[d][:, h, :], AF.Copy, [SST[d].res], [], parts=[SBA[d].res])
                        if s_ == NCH - 1:
                            continue
                        ub = PB[(2 * s_ + d) % 6]
                        P.add("pe", lambda e, ub=ub, d=d, c=c: e.matmul(ub[:, 0:256], lhsT=HKH[d][:, c, :], rhs=HV[:, c, :], start=True, stop=True),
                              reads=[HKH[d].res, HV.res], parts=[ub.res])
                        P.add("dve", lambda e, ub=ub, d=d, c=c, h=h: e.scalar_tensor_tensor(
                            out=SST[d][:, h, :], in0=SST[d][:, h, :], scalar=DEC[:, d, h, c:c + 1], in1=ub[:, 0:256], op0=ALU.mult, op1=ALU.add),
                            reads=[DEC.res, SST[d].res], writes=[ub.res], parts=[SST[d].res])
                for s_ in range(NCH):
                    ada_pump((16 + NCH - 1) // NCH)
                    cc = [s_, NCH - 1 - s_]
                    css = [slice(c * 128, (c + 1) * 128) for c in cc]
                    for d in range(2):
                        ab = PB[d]
                        P.add("pe", lambda e, ab=ab, d=d, cs=css[d]: e.matmul(ab[:, 0:128], lhsT=HQ[1][d][:, cs], rhs=HQ[0][d][:, cs], start=True, stop=True),
                              reads=[HQ[0][d].res, HQ[1][d].res], parts=[ab.res])
                    for d in range(2):
                        ab = PB[d]
                        P.add("dve", lambda e, ab=ab, d=d: e.tensor_tensor(out=ATT[d].ap, in0=ab[:, 0:128], in1=MK[d], op=ALU.mult),
                              reads=[CONSTS.res], writes=[ATT[d].res, ab.res])
                    for d in range(2):
                        ob = PB[2 + d]
                        P.add("pe", lambda e, ob=ob, d=d, c=cc[d]: e.matmul(ob[:, 0:256], lhsT=ATT[d].ap, rhs=HV[:, c, :], start=True, stop=False),
                              reads=[ATT[d].res, HV.res], parts=[ob.res])
                        P.add("pe", lambda e, ob=ob, d=d, cs=css[d], c=cc[d]: e.matmul(ob[:, 0:256], lhsT=HQ[0][d][:, cs], rhs=SBA[d][:, c, :], start=False, stop=True),
                              reads=[HQ[0][d].res, SBA[d].res], parts=[ob.res])
                    for d in range(2):
                        ob = PB[2 + d]
                        c = cc[d]
                        cs = css[d]
                        if s_ < NCH // 2:
                            act(OST[:, c, :], ob[:, 0:256], AF.Copy, [], [ob.res], parts=[OST.res])
                        else:
                            P.add("dve", lambda e, ob=ob, c=c, d=d: e.tensor_tensor(out=OSUM[d].ap, in0=ob[:, 0:256], in1=OST[:, c, :], op=ALU.add),
                                  reads=[OST.res], writes=[OSUM[d].res, ob.res])
                            P.add("dve", lambda e, d=d: e.memset(RS[d][:, 0:1], 0.0), writes=[RS[d].res])
                            act(JNK[d].ap, OSUM[d].ap, AF.Square, [OSUM[d].res, RS[d].res], [JNK[d].res], accum=RS[d][:, 0:1], parts=[RS[d].res])
                            act(RS[d][:, 1:2], RS[d][:, 0:1], AF.Ln, [RS[d].res, CV.res], [], scale=1.0 / 256, bias=EPSC, parts=[RS[d].res])
                            act(RS[d][:, 1:2], RS[d][:, 1:2], AF.Exp, [RS[d].res], [], scale=-0.5, parts=[RS[d].res])
                            P.add("dve", lambda e, h=h, d=d: e.scalar_tensor_tensor(out=OSUM[d].ap, in0=OSUM[d].ap, scalar=RS[d][:, 1:2], in1=GLAG[:, h * 256:(h + 1) * 256],
                                                                                     op0=ALU.mult, op1=ALU.mult), reads=[OSUM[d].res, RS[d].res, GLAG.res], writes=[OSUM[d].res])
                            P.add("dve", lambda e, c=c, d=d: e.tensor_tensor(out=YT[d].ap, in0=OSUM[d].ap, in1=HR[:, c, :], op=ALU.mult),
                                  reads=[OSUM[d].res, HR.res], writes=[YT[d].res])
                            tbk = 6
                            for b2 in range(2):
                                P.add("pe", lambda e, tbk=tbk, b2=b2, d=d: e.transpose(PBH[tbk][:, d * 256 + b2 * 128: d * 256 + (b2 + 1) * 128], YT[d][:, b2 * 128:(b2 + 1) * 128], IDB.ap),
                                      reads=[YT[d].res, IDB.res], parts=[PB[tbk].res])
                            P.add("dve", lambda e, tbk=tbk, cs=cs, d=d: e.tensor_copy(out=YCT[:, :, cs], in_=PBH[tbk][:, d * 256:(d + 1) * 256].rearrange("p (b c) -> p b c", b=2)),
                                  writes=[PB[tbk].res], parts=[YCT.res])
                dma(CATT, CATT[h * 256:(h + 1) * 256, :].rearrange("(b p) t -> p b t", p=128), YCT, YCT.ap, key="spill2")

        NPREC = 32 if NT // T > 1 else 0
        PSLOT = [Res(f"pslot{i}") for i in range(4)]

        def precast_steps():
            blk = 0
            specs = []
            for cb in range(8):
                for half in range(2):
                    specs.append(("tm", w_o, half * 2048, cb * 512))
            for b8 in range(8):
                specs.append(("fm", w_1, 0, b8 * 256))
            for cb in range(8):
                specs.append(("tm", w_2, 0, cb * 512))
            for (kind_, w, r0, c0) in specs[:NPREC]:
                WSCR = WSCRS[blk // 72]
                for pc in range(2):
                    if kind_ == "tm":
                        src = w[r0 + pc * 1024: r0 + (pc + 1) * 1024, c0:c0 + 512].rearrange("(a p) n -> p a n", p=128)
                        dst = WSCR[blk % 72, :, pc * 4096:(pc + 1) * 4096].rearrange("p (a n) -> p a n", a=8)
                    else:
                        src = w[pc * 2048:(pc + 1) * 2048, c0:c0 + 256].rearrange("(a p) n -> p a n", p=128)
                        dst = WSCR[blk % 72, :, pc * 4096:(pc + 1) * 4096].rearrange("p (a n) -> p a n", a=16)
                    slot = PSLOT[(2 * blk + pc) % 4]
                    P.add("pool", lambda e, dst=dst, src=src: e.dma_start(out=dst, in_=src), writes=[slot], parts=[WSCR.res],
                          key="pslot%d" % ((2 * blk + pc) % 4))
                blk += 1
                yield

        pc_it = precast_steps()
        load_head(0)
        for h in range(8):
            if h + 1 < 8:
                load_head(h + 1)
            for _ in range(4):
                try:
                    next(pc_it)
                except StopIteration:
                    break
            head_body(h, HQ2[h % 2], HKH2[h % 2], HV2[h % 2], HR2[h % 2])
        for _ in pc_it:
            pass
        P.barrier()
        release(m2)

        release(base3_mark)
        m3 = mark()
        MM = sb("mm", [128, 4, D], F32)
        XB3 = sb("xblk3", [128, D], F32)
        H2T = sb("h2T", [128, KC, T], BF16)
        CT = sb("ct", [128, KC, T], BF16)
        HID = Tl(CT[:, 0:16, :], "hid")
        HID.res = CT.res
        XN3 = Tl(CT.ap.rearrange("p a b -> p (a b)")[:, 8192:8192 + D], "xn3")
        XN3.res = CT.res
        WB[:] = [sb(f"wb3{i}", [128, 8192], BF16) for i in range(NWB)]
        G1B = sb("g1b", [128, D], BF16)
        G2B = sb("g2b", [128, D], BF16)
        RT13 = sb("rt13", [128, T], F32)
        SS3 = sb("ss3", [128, 8], F32)
        for (G, goff, post) in ((G1B, 2 * D, post1_d), (G2B, 5 * D, post2_d)):
            dma(MM, MM[:, 0, :], MODROW, MODROW[0:1, goff:goff + D].partition_broadcast(128), part=False, key="mmld")
            dma(XB3, XB3.ap, None, post.partition_broadcast(128), part=False)
            P.add("dve", lambda e, G=G: e.tensor_tensor(out=G.ap, in0=MM[:, 0, :], in1=XB3.ap, op=ALU.mult), reads=[MM.res, XB3.res], writes=[G.res])

        def rstd_of(src_ap, src_res, col):
            P.add("dve", lambda e: e.memset(SS3[:, col:col + 1], 0.0), parts=[SS3.res])
            act(XN3.ap, src_ap, AF.Square, [src_res, SS3.res], [XN3.res], accum=SS3[:, col:col + 1], parts=[SS3.res])
            act(SS3[:, 4 + col:5 + col], SS3[:, col:col + 1], AF.Ln, [SS3.res, CV.res], [SS3.res], scale=1.0 / D, bias=EPSC)
            act(SS3[:, 4 + col:5 + col], SS3[:, 4 + col:5 + col], AF.Exp, [SS3.res], [SS3.res], scale=-0.5)
            return SS3[:, 4 + col:5 + col]

        def tm_matmul(w, nhalf, rowbase, colbase, lhs_of, first, last):
            for half in range(nhalf):
                wb, view = load_tm(w, rowbase + half * 2048, colbase)
                for tb in range(4):
                    bank = PB[tb]
                    for k16 in range(16):
                        kk = half * 16 + k16
                        P.add("pe", lambda e, bank=bank, view=view, tb=tb, kk=kk, k16=k16: e.matmul(
                            bank.ap, lhsT=lhs_of(kk, tb), rhs=view[:, k16, :], start=(first and kk == 0), stop=(last and kk == nhalf * 16 - 1),
                            skip_group_check=True), reads=[H2T.res, CT.res, wb.res], parts=[bank.res])

        for ti in range(NT // T):
            wscr["mode"] = ("wb" if ti == 0 else "rd") if NT // T > 1 else None
            wscr["nprec"] = NPREC
            wscr["blk"] = 0
            dma(CT, CT.ap, CATT, CATT[:, ti * T:(ti + 1) * T].rearrange("(c p) t -> p c t", p=128), part=False)
            for cb in range(8):
                tm_matmul(w_o, 2, 0, cb * 512, lambda kk, tb: CT[:, kk, tb * 128:(tb + 1) * 128], True, True)
                for tb in range(4):
                    act(MM[:, tb, cb * 512:(cb + 1) * 512], PB[tb].ap, AF.Copy, [], [PB[tb].res], parts=[MM.res])
            for tb in range(4):
                r0 = ti * T + tb * 128
                rs = rstd_of(MM[:, tb, :], MM.res, tb)
                dma(XB3, XB3.ap, None, x_own[r0:r0 + 128, :], part=False)
                P.add("dve", lambda e, tb=tb, rs=rs: e.scalar_tensor_tensor(out=MM[:, tb, :], in0=MM[:, tb, :], scalar=rs, in1=G1B.ap, op0=ALU.mult, op1=ALU.mult),
                      reads=[MM.res, SS3.res, G1B.res], parts=[MM.res])
                P.add("dve", lambda e, tb=tb: e.tensor_tensor(out=XB3.ap, in0=XB3.ap, in1=MM[:, tb, :], op=ALU.add), reads=[XB3.res, MM.res], writes=[XB3.res])
                dma(X1, X1[r0:r0 + 128, :], XB3, XB3.ap, key="x1st")
                rs2 = rstd_of(XB3.ap, XB3.res, tb)
                P.add("dve", lambda e, rs2=rs2: e.tensor_scalar(out=XN3.ap, in0=XB3.ap, scalar1=rs2, scalar2=None, op0=ALU.mult),
                      reads=[XB3.res, SS3.res], writes=[XN3.res])
                for k2 in range(KC // 8):
                    bi = 6 + (k2 % 2)
                    for j in range(8):
                        kc = k2 * 8 + j
                        P.add("pe", lambda e, bi=bi, j=j, kc=kc: e.transpose(PBH[bi][:, j * 128:(j + 1) * 128], XN3[:, kc * 128:(kc + 1) * 128], IDB.ap),
                              reads=[XN3.res, IDB.res], parts=[PB[bi].res])
                    for j in range(8):
                        kc = k2 * 8 + j
                        P.add("dve", lambda e, bi=bi, j=j, kc=kc, tb=tb: e.tensor_scalar(
                            out=H2T[:, kc, tb * 128:(tb + 1) * 128], in0=PBH[bi][:, j * 128:(j + 1) * 128], scalar1=A2[:, kc:kc + 1], scalar2=SH2[:, kc:kc + 1],
                            op0=ALU.mult, op1=ALU.add), reads=[MODT.res, A2.res], writes=[PB[bi].res], parts=[H2T.res])
            for grp in range(8):
                for b8 in range(8):
                    wb, view = load_fm(w_1, grp * 2048 + b8 * 256)
                    for j in range(2):
                        hc = b8 * 2 + j
                        bank = PB[4 + j]
                        mm_fm(view, j, bank, T, wb, H2T)
                        act(RT13.ap, bank.ap, AF.Relu, [], [RT13.res, bank.res])
                        P.add("dve", lambda e, hc=hc: e.tensor_tensor(out=HID[:, hc, :], in0=RT13.ap, in1=RT13.ap, op=ALU.mult), reads=[RT13.res], parts=[HID.res])
                for cb in range(8):
                    tm_matmul(w_2, 1, grp * 2048, cb * 512, lambda kk, tb: HID[:, kk, tb * 128:(tb + 1) * 128], True, True)
                    for tb in range(4):
                        if grp == 0:
                            act(MM[:, tb, cb * 512:(cb + 1) * 512], PB[tb].ap, AF.Copy, [], [PB[tb].res], parts=[MM.res])
                        else:
                            P.add("dve", lambda e, tb=tb, cb=cb: e.tensor_tensor(out=MM[:, tb, cb * 512:(cb + 1) * 512], in0=PB[tb].ap,
                                                                                 in1=MM[:, tb, cb * 512:(cb + 1) * 512], op=ALU.add),
                                  reads=[MM.res], writes=[PB[tb].res], parts=[MM.res])
            for tb in range(4):
                r0 = ti * T + tb * 128
                rs = rstd_of(MM[:, tb, :], MM.res, tb)
                dma(XB3, XB3.ap, X1, X1[r0:r0 + 128, :], part=False)
                P.add("dve", lambda e, tb=tb, rs=rs: e.scalar_tensor_tensor(out=MM[:, tb, :], in0=MM[:, tb, :], scalar=rs, in1=G2B.ap, op0=ALU.mult, op1=ALU.mult),
                      reads=[MM.res, SS3.res, G2B.res], parts=[MM.res])
                P.add("dve", lambda e, tb=tb: e.tensor_tensor(out=XB3.ap, in0=XB3.ap, in1=MM[:, tb, :], op=ALU.add), reads=[XB3.res, MM.res], writes=[XB3.res])
                dma(y_out, y_out[r0:r0 + 128, :], XB3, XB3.ap, key="yst")
        P.add("sp", None, reads=[y_out.res])
        P.emit(nc, st)
    return nc


_NC_CACHE = {}


def _vecT(v):
    return np.ascontiguousarray(v.reshape(-1, 128).T)


def _rope_tables(pos):
    inv = (np.float32(10000.0) ** (-np.arange(32, dtype=np.float32) / np.float32(32))).astype(np.float32)
    row = (pos // 64).astype(np.float32)
    col = (pos % 64).astype(np.float32)
    ang = np.empty((128, pos.shape[0]), np.float32)
    for ch in range(128):
        base = row if ch < 64 else col
        ang[ch] = base * inv[ch % 32]
    out = np.empty((128, 2, pos.shape[0]), np.float32)
    out[:, 0] = np.cos(ang)
    out[:, 1] = np.sin(ang)
    return out


def _consts():
    c = np.zeros((128, 6, 128), np.float32)
    c[:, 0] = np.eye(128, dtype=np.float32)
    Pm = np.zeros((128, 128), np.float32)
    for base in (0, 64):
        for i in range(32):
            Pm[base + 32 + i, base + i] = -1.0
            Pm[base + i, base + 32 + i] = 1.0
    c[:, 1] = Pm
    j = np.arange(128)[:, None]
    i = np.arange(128)[None, :]
    c[:, 2] = (j <= i)
    c[:, 3] = (j >= i)
    c[:, 4] = (j > i)
    c[:, 5] = (j < i)
    return c


def make_in_maps(inputs):
    x = np.asarray(inputs["x"], np.float32)
    c = np.asarray(inputs["c"], np.float32)
    ctx = np.asarray(inputs["ctx"], np.float32)
    c_ctx = np.asarray(inputs["c_ctx"], np.float32)
    g = lambda k: np.asarray(inputs[k], np.float32)[0]
    w_ada, b_ada, w_in = g("w_ada"), g("b_ada"), g("w_in")
    w_o, w_1, w_2 = g("w_o"), g("w_1"), g("w_2")
    wdf, bdf, wdb, bdb = g("w_dec_f"), g("b_dec_f"), g("w_dec_b"), g("b_dec_b")
    w_s, b_s = g("w_s"), g("b_s")
    w_in_sw = w_in.copy()
    w_in_sw[:, LF0:LF0 + 16] = w_in[:, LB0:LB0 + 16]
    w_in_sw[:, LB0:LB0 + 16] = w_in[:, LF0:LF0 + 16]
    consts = _consts()
    pos_all = np.arange(2 * NT)
    shared = dict(w_o=w_o, w_1=w_1, w_2=w_2, w_ada=w_ada, bada=b_ada.reshape(1, -1),
                  pre1T=_vecT(g("pre1_g")), pre2T=_vecT(g("pre2_g")),
                  post1=g("post1_g").reshape(1, -1), post2=g("post2_g").reshape(1, -1),
                  glag=g("gla_norm_g").reshape(1, -1), lng=g("sg_ln_g").reshape(1, -1), lnb=g("sg_ln_b").reshape(1, -1),
                  consts=consts)
    maps = []
    for b in range(4):
        for h in range(2):
            m = dict(shared)
            if h == 0:
                xs, cx, pos = x[b], ctx[b], pos_all
                m["w_in"] = w_in
                wl, bl = (wdf, wdb), (bdf, bdb)
                ws, bs = w_s, b_s
            else:
                xs, cx, pos = x[b, ::-1], ctx[b, ::-1], pos_all[::-1]
                m["w_in"] = w_in_sw
                wl, bl = (wdb, wdf), (bdb, bdf)
                ws, bs = w_s[:, ::-1, ::-1], b_s[:, ::-1]
            m["x_own"] = np.ascontiguousarray(xs[:NT])
            m["x_pre"] = np.ascontiguousarray(xs[NT:])
            m["x_ctx"] = np.ascontiguousarray(cx)
            craw = np.empty((128, KC, 2), np.float32)
            craw[:, :, 0] = _vecT(c[b])
            craw[:, :, 1] = _vecT(c_ctx)
            m["craw"] = craw
            wd = np.zeros((33, 4, 512), np.float32)
            for p in range(4):
                for d in range(2):
                    for hh in range(2):
                        hd = 2 * p + hh
                        sl = slice((d * 2 + hh) * 128, (d * 2 + hh + 1) * 128)
                        wd[d * 16:(d + 1) * 16, p, sl] = wl[d][:, hd * 128:(hd + 1) * 128]
                        wd[32, p, sl] = bl[d][hd * 128:(hd + 1) * 128]
            m["wdec"] = wd
            m["wsT"] = np.ascontiguousarray(np.transpose(ws, (2, 0, 1)))
            m["bs"] = np.ascontiguousarray(np.tile(bs[:, None, :], (1, 4, 1)).reshape(1, 4, 512))
            m["cs_own"] = _rope_tables(pos[:NT])
            m["cs_pre"] = _rope_tables(pos[NT:])
            maps.append(m)
    return maps


def kernel(**inputs):
    if "nc" not in _NC_CACHE:
        _NC_CACHE["nc"] = build()
    nc = _NC_CACHE["nc"]
    maps = make_in_maps(inputs)
    res = run_bass_kernel_spmd(nc, maps, core_ids=list(range(8)))
    out = np.empty((4, 4096, D), np.float32)
    for b in range(4):
        for h in range(2):
            y = np.asarray(res.results[2 * b + h]["y"])
            if h == 0:
                out[b, :NT] = y
            else:
                out[b, NT:] = y[::-1]
    return out
```

```python
import os
import numpy as np
from contextlib import ExitStack
import concourse.bass as bass
import concourse.mybir as mybir
from concourse.bass_utils import run_bass_kernel_spmd

F32 = mybir.dt.float32
BF16 = mybir.dt.bfloat16
AF = mybir.ActivationFunctionType
ALU = mybir.AluOpType
AX = mybir.AxisListType

D = 4096
NT = int(os.environ.get("KSIM_NT", "2048"))
NPRE = NT
NCH = NT // 128
NCTX = 256
T = 512
KC = D // 128
DFF = 4 * D
IN_COLS = 10272
Q0, K0, V0, R0, LF0, LB0, SG0 = 0, 1024, 2048, 4096, 6144, 6160, 6176
U0, VV0 = SG0, SG0 + 2048
EPS = 1e-6
SEM_LIMIT = 24000
NWB = 2


class Res:
    __slots__ = ("name", "writers", "readers")

    def __init__(self, name):
        self.name = name
        self.writers = []
        self.readers = []


class Op:
    __slots__ = ("eng", "fn", "deps", "key", "ticket", "sem", "signal", "idx")


class Prog:
    ENGS = ("pe", "act", "dve", "pool", "sp")

    def __init__(self):
        self.ops = {e: [] for e in self.ENGS}
        self.keys = {}
        self.nops = 0
        self.pending = {e: [] for e in self.ENGS}
        self.lastc = {}

    def barrier(self):
        last = [op for op in self.lastc.values()]
        for k, lst in self.keys.items():
            last.append(lst[-1])
        for e in self.ENGS:
            self.pending[e] = list(last)

    def add(self, eng, fn, reads=(), writes=(), parts=(), key=None):
        op = Op()
        op.eng = eng
        op.fn = fn
        op.key = key
        op.signal = False
        op.ticket = None
        op.sem = None
        op.idx = self.nops
        self.nops += 1
        raw, other = [], []
        for r in reads:
            raw.extend(r.writers)
        for r in writes:
            other.extend(r.writers)
            other.extend(r.readers)
        for r in parts:
            if r.readers:
                other.extend(r.readers)
                other.extend(r.writers)
            else:
                for w in r.writers:
                    same = (w.key is None and key is None and w.eng == eng and eng == 'pe') or (w.key is not None and w.key == key)
                    if not same:
                        other.append(w)
        deps = {}
        same_ok = eng in ("act", "dve", "pool")

        def consider(d, is_raw):
            if d is op:
                return
            if d.key is None and d.eng == eng and key is None:
                if not same_ok:
                    return
            k = ("k", d.key) if d.key is not None else ("e", d.eng)
            cur = deps.get(k)
            if cur is None or d.idx > cur.idx:
                deps[k] = d

        for d in raw:
            consider(d, True)
        for d in other:
            consider(d, False)
        if self.pending[eng]:
            for d in self.pending[eng]:
                if not (d.key is None and d.eng == eng):
                    consider(d, True)
            self.pending[eng] = []
        op.deps = list(deps.values())
        for d in op.deps:
            d.signal = True
        for r in reads:
            r.readers.append(op)
        for r in writes:
            r.writers = [op]
            r.readers = []
        for r in parts:
            if r.readers:
                r.writers = [op]
                r.readers = []
            else:
                r.writers.append(op)
        self.ops[eng].append(op)
        if key is not None:
            self.keys.setdefault(key, []).append(op)
        elif fn is not None:
            self.lastc[eng] = op
        return op

    def emit(self, nc, stack):
        def newsem(nm):
            return stack.enter_context(nc.semaphore(nm))

        for e in self.ENGS:
            cnt, epoch, cur = 0, 0, None
            for op in self.ops[e]:
                if op.key is not None or not op.signal:
                    continue
                if cur is None or cnt >= SEM_LIMIT:
                    cur = newsem(f"s_{e}_{epoch}")
                    epoch += 1
                    cnt = 0
                cnt += 1
                op.sem = cur
                op.ticket = cnt
        for k, lst in self.keys.items():
            s = newsem(f"k_{k}")
            cnt = 0
            for op in lst:
                cnt += 16
                op.sem = s
                op.ticket = cnt
        handles = {"pe": "tensor", "act": "scalar", "dve": "vector", "pool": "gpsimd", "sp": "sync"}
        block = stack.enter_context(nc.Block())

        def make(e):
            oplist = self.ops[e]

            def body(eng):
                waited = {}
                for op in oplist:
                    for d in op.deps:
                        sid = id(d.sem)
                        if waited.get(sid, 0) >= d.ticket:
                            continue
                        waited[sid] = d.ticket
                        eng.wait_ge(d.sem, d.ticket)
                    if op.fn is None:
                        continue
                    ins = op.fn(eng)
                    if op.key is not None:
                        ins.then_inc(op.sem, 16)
                    elif op.signal:
                        ins.then_inc(op.sem, 1)
            return body

        for e in self.ENGS:
            if self.ops[e]:
                getattr(block, handles[e])(make(e))


class Tl:
    __slots__ = ("ap", "res")

    def __init__(self, ap, name):
        self.ap = ap
        self.res = Res(name)

    def __getitem__(self, k):
        return self.ap[k]


def build(dbg=False):
    nc = bass.Bass("TRN2", target_bir_lowering=False)
    P = Prog()

    def din(name, shape, dt=F32):
        return nc.dram_tensor(name, list(shape), dt, kind="ExternalInput").ap()

    def dscr(name, shape, dt):
        kind = "ExternalOutput" if (dbg and name in ("CATT", "QF", "KHB", "DBG")) else "Internal"
        return Tl(nc.dram_tensor(name, list(shape), dt, kind=kind).ap(), name)

    x_own = din("x_own", [NT, D])
    x_pre = din("x_pre", [NPRE, D])
    x_ctx = din("x_ctx", [NCTX, D])
    w_in = din("w_in", [D, IN_COLS])
    w_o = din("w_o", [D, D])
    w_1 = din("w_1", [D, DFF])
    w_2 = din("w_2", [DFF, D])
    w_ada = din("w_ada", [D, 6 * D])
    craw_d = din("craw", [128, KC, 2])
    bada_d = din("bada", [1, 6 * D])
    pre1T_d = din("pre1T", [128, KC])
    pre2T_d = din("pre2T", [128, KC])
    post1_d = din("post1", [1, D])
    post2_d = din("post2", [1, D])
    glag_d = din("glag", [1, 2048])
    lng_d = din("lng", [1, 2048])
    lnb_d = din("lnb", [1, 2048])
    wdec_d = din("wdec", [33, 4, 512])
    wsT_d = din("wsT", [128, 4, 128])
    bs_d = din("bs", [1, 4, 512])
    cs_own_d = din("cs_own", [128, 2, NT])
    cs_pre_d = din("cs_pre", [128, 2, NPRE])
    consts_d = din("consts", [128, 6, 128])
    y_out = Tl(nc.dram_tensor("y", [NT, D], F32, kind="ExternalOutput").ap(), "y")

    QD = [[dscr(f"{n}{d}", [8, 128, NT], BF16) for d in "FB"] for n in "QK"]
    KH = [dscr(f"KH{d}", [8, 128, NCH, 128], BF16) for d in "FB"]
    VD = dscr("VD", [NT, 2048], BF16)
    RD = dscr("RD", [NT, 2048], BF16)
    CATT = dscr("CATT", [D, NT], BF16)
    X1 = dscr("X1", [NT, D], F32)
    MODROW = dscr("MODROW", [2, 6 * D], F32)
    DBG = dscr("DBG", [128, 4096], F32)
    WSCRS = [dscr(f"WSCR{i}", [72, 128, 8192], BF16) for i in range(2)]

    st = ExitStack()
    with st:
        ARENA_B = 200 * 1024
        arena = st.enter_context(nc.sbuf_tensor("arena", [128, ARENA_B // 2], BF16))
        state = {"off": 0}

        def mark():
            return state["off"]

        def release(m):
            state["off"] = m

        def sb(name, shape, dt):
            esz = 4 if dt == F32 else 2
            n = int(np.prod(shape[1:]))
            nb = (n * esz + 63) // 64 * 64
            off = state["off"]
            assert off + nb <= ARENA_B, (name, off, nb)
            state["off"] = off + nb
            state["max"] = max(state.get("max", 0), off + nb)
            if os.environ.get("KDBG_ALLOC"):
                print("ALLOC", name, off, nb, flush=True)
            offs[name] = off
            v = arena[0:shape[0], off // 2:(off + n * esz) // 2]
            if dt == F32:
                v = v.bitcast(F32)
            if len(shape) == 3:
                v = v.rearrange("p (a b) -> p a b", a=shape[1])
            elif len(shape) == 4:
                v = v.rearrange("p (a b c) -> p a b c", a=shape[1], b=shape[2])
            return Tl(v, name)

        offs = {}

        def arena_view(tile, nbytes, dt):
            off = offs[tile.res.name]
            v = arena[:, off // 2:(off + nbytes) // 2]
            return v.bitcast(F32) if dt == F32 else v

        banks = [st.enter_context(nc.psum_tensor(f"pb{i}", [128, 512], F32)) for i in range(8)]
        PB = [Tl(b[:], f"pb{i}") for i, b in enumerate(banks)]
        PBH = [b[:].bitcast(BF16) for b in banks]

        bank_rr = {"i": 0}

        def nextbank(lo=0, hi=6):
            i = bank_rr["i"]
            bank_rr["i"] = (i + 1 - lo) % (hi - lo) + lo if lo <= i < hi else lo
            return i if lo <= i < hi else lo

        dma_rr = {"i": 0}

        def dma(out_t, out_ap, in_t, in_ap, eng="sp", key=None, part=True):
            k = key if key is not None else out_t.res.name
            if k == "spill":
                k = "sp_" + in_t.res.name
            P.add(eng, lambda e: e.dma_start(out=out_ap, in_=in_ap),
                  reads=[in_t.res] if in_t is not None else [],
                  parts=[out_t.res] if part else [], writes=[] if part else [out_t.res], key=k)

        CONSTS = sb("consts", [128, 6, 128], F32)
        dma(CONSTS, CONSTS.ap, None, consts_d)
        IDF = CONSTS[:, 0, :]
        ROPEP = CONSTS[:, 1, :]
        MK = [CONSTS[:, 2 + i, :] for i in range(4)]
        IDB = sb("idb", [128, 128], BF16)
        P.add("dve", lambda e: e.tensor_copy(out=IDB.ap, in_=IDF), reads=[CONSTS.res], writes=[IDB.res])
        CV = sb("cvals", [128, 4], F32)
        P.add("dve", lambda e: e.memset(CV[:, 0:1], 1.0), parts=[CV.res])
        P.add("dve", lambda e: e.memset(CV[:, 1:2], EPS), parts=[CV.res])
        P.add("dve", lambda e: e.memset(CV[:, 2:3], 0.0), parts=[CV.res])
        ONE = CV[:, 0:1]
        EPSC = CV[:, 1:2]
        ONESF = sb("onesf", [128, 128], F32)
        P.add("dve", lambda e: e.memset(ONESF.ap, 1.0), writes=[ONESF.res])
        ONESB = sb("onesb", [128, 128], BF16)
        P.add("dve", lambda e: e.memset(ONESB.ap, 1.0), writes=[ONESB.res])
        MODT = sb("modT", [128, 192], F32)
        MODCT = sb("modcT", [128, 64], F32)
        A1 = sb("A1", [128, KC], F32)
        A1C = sb("A1C", [128, KC], F32)
        A2 = sb("A2", [128, KC], F32)
        PRE1T = sb("pre1T", [128, KC], F32)
        PRE2T = sb("pre2T", [128, KC], F32)
        dma(PRE1T, PRE1T.ap, None, pre1T_d)
        dma(PRE2T, PRE2T.ap, None, pre2T_d)
        base3_mark = mark()
        DEC = sb("DEC", [128, 2, 8, NCH], F32)
        SST = [sb(f"S{d}", [128, 8, 256], F32) for d in range(2)]
        WDEC = sb("wdecs", [33, 4, 512], F32)
        dma(WDEC, WDEC.ap, None, wdec_d)
        WLR = sb("wlr", [128, KC, 32], BF16)
        dma(WLR, WLR.ap, None, w_in[:, LF0:LF0 + 32].rearrange("(c p) n -> p c n", p=128), eng="pool")
        WST = sb("wst", [128, 4, 128], BF16)
        dma(WST, WST.ap, None, wsT_d, eng="pool")
        BS2 = sb("bs2", [2, 4, 512], BF16)

        def act(out_ap, in_ap, func, reads, writes, scale=1.0, bias=None, accum=None, parts=()):
            kw = {}
            if bias is not None:
                kw["bias"] = bias
            if accum is not None:
                kw["accum_out"] = accum
            P.add("act", lambda e: e.activation(out=out_ap, in_=in_ap, func=func, scale=scale, **kw),
                  reads=reads, writes=writes, parts=parts)

        mtmp = mark()
        BSF = sb("bsf", [1, 4, 512], F32)
        BSL = sb("bsl", [1, 4, 512], F32)
        BSLB = sb("bslb", [1, 4, 512], BF16)
        dma(BSF, BSF.ap, None, bs_d)
        P.add("dve", lambda e: e.tensor_copy(out=BS2[0:1, :, :], in_=BSF.ap), reads=[BSF.res], parts=[BS2.res])
        P.add("dve", lambda e: e.tensor_copy(out=BSL.ap, in_=BS2[0:1, :, :]), reads=[BS2.res], writes=[BSL.res])
        P.add("dve", lambda e: e.tensor_tensor(out=BSL.ap, in0=BSF.ap, in1=BSL.ap, op=ALU.subtract),
              reads=[BSF.res, BSL.res], writes=[BSL.res])
        P.add("dve", lambda e: e.tensor_copy(out=BSLB.ap, in_=BSL.ap), reads=[BSL.res], writes=[BSLB.res])
        dma(BS2, BS2[1:2, :, :], BSLB, BSLB.ap)
        P.barrier()
        release(mtmp)

        CRAW = sb("craw", [128, KC, 2], F32)
        dma(CRAW, CRAW.ap, None, craw_d)
        act(CRAW.ap, CRAW.ap, AF.Silu, [CRAW.res], [CRAW.res])
        base_mark = mark()

        class Ada:
            def __init__(self, nwa=3):
                self.nwa = nwa
                self.CONDB = sb("condB", [128, KC, 128], F32)
                self.WA = [sb(f"wa{i}", [128, 8, 512], F32) for i in range(nwa)]
                self.BROW = sb("brow", [128, 512], F32)
                self.MB = sb("mb", [128, 512], F32)
                self.wq = 0
                CONDB = self.CONDB
                for kc in range(KC):
                    P.add("dve", lambda e, kc=kc: e.tensor_scalar(out=CONDB[:, kc, 0:64], in0=ONESF[:, 0:64], scalar1=CRAW[:, kc, 0:1],
                                                                  scalar2=None, op0=ALU.mult), reads=[CRAW.res, ONESF.res], parts=[CONDB.res])
                    P.add("dve", lambda e, kc=kc: e.tensor_scalar(out=CONDB[:, kc, 64:128], in0=ONESF[:, 0:64], scalar1=CRAW[:, kc, 1:2],
                                                                  scalar2=None, op0=ALU.mult), reads=[CRAW.res, ONESF.res], parts=[CONDB.res])

            def steps(self, cbs, banks):
                CONDB, BROW, MB = self.CONDB, self.BROW, self.MB
                for n, cb in enumerate(cbs):
                    bk = PB[banks[0] + (n % 2 if banks[2] else 0)]
                    for q4 in range(4):
                        wa = self.WA[self.wq % self.nwa]
                        self.wq += 1
                        dma(wa, wa.ap, None, w_ada[q4 * 1024:(q4 + 1) * 1024, cb * 512:(cb + 1) * 512].rearrange("(c p) n -> p c n", p=128), part=False)
                        for k8 in range(8):
                            kc = q4 * 8 + k8
                            P.add("pe", lambda e, bk=bk, wa=wa, k8=k8, kc=kc: e.matmul(bk.ap, lhsT=CONDB[:, kc, :], rhs=wa[:, k8, :],
                                                                                       start=(kc == 0), stop=(kc == KC - 1)),
                                  reads=[CONDB.res, wa.res], parts=[bk.res])
                        if q4 < 3:
                            yield
                    dma(BROW, BROW.ap, None, bada_d[0:1, cb * 512:(cb + 1) * 512].partition_broadcast(128), part=False)
                    P.add("dve", lambda e, bk=bk: e.tensor_tensor(out=MB.ap, in0=bk.ap, in1=BROW.ap, op=ALU.add),
                          reads=[BROW.res], writes=[MB.res, bk.res])
                    dma(MODROW, MODROW[0:1, cb * 512:(cb + 1) * 512], MB, MB[0:1, :], key="modrow")
                    tb = PB[banks[1] + (n % 2 if banks[2] else 0)]
                    for j in range(4):
                        P.add("pe", lambda e, tb=tb, j=j: e.transpose(tb[:, j * 128:(j + 1) * 128], MB[:, j * 128:(j + 1) * 128], IDF),
                              reads=[MB.res, CONSTS.res], parts=[tb.res])
                    tv = tb.ap.rearrange("p (j c) -> p j c", j=4)
                    P.add("dve", lambda e, tv=tv, cb=cb: e.tensor_copy(out=MODT[:, cb * 4:(cb + 1) * 4], in_=tv[:, :, 0]),
                          parts=[MODT.res], writes=[tb.res])
                    if cb < 16:
                        P.add("dve", lambda e, tv=tv, cb=cb: e.tensor_copy(out=MODCT[:, cb * 4:(cb + 1) * 4], in_=tv[:, :, 64]),
                              parts=[MODCT.res], writes=[tb.res])
                    yield

        m0 = mark()
        ada0 = Ada()
        for _ in ada0.steps(range(48), (0, 2, True)):
            pass
        for (dst, mod, off, pre) in ((A1, MODT, 32, PRE1T), (A1C, MODCT, 32, PRE1T), (A2, MODT, 128, PRE2T)):
            P.add("dve", lambda e, dst=dst, mod=mod, off=off, pre=pre: e.scalar_tensor_tensor(
                out=dst.ap, in0=mod[:, off:off + KC], scalar=1.0, in1=pre.ap, op0=ALU.add, op1=ALU.mult),
                reads=[mod.res, pre.res], writes=[dst.res])
        SH1 = MODT[:, 0:KC]
        SH1C = MODCT[:, 0:KC]
        SH2 = MODT[:, 96:128]
        P.barrier()
        release(m0)

        m1 = mark()
        HXT = sb("hxT", [128, KC, T], BF16)
        WB = [sb(f"wb{i}", [128, 8192], BF16) for i in range(NWB)]
        XB = sb("xblk", [128, D], F32)
        XN = sb("xn", [128, D], BF16)
        SSs = [sb(f"ss{i}", [128, 2], F32) for i in range(4)]
        LRT = sb("lrT", [33, T], F32)
        P.add("dve", lambda e: e.memset(LRT[32:33, :], 1.0), parts=[LRT.res])
        CS = sb("cs", [128, 2, T], F32)
        QRAW = sb("qraw", [128, T], F32)
        RT1 = sb("rt1", [128, T], F32)
        QKR = sb("qkr", [128, 4, T], F32)
        STG = sb("stg", [128, 8, T], BF16)
        SP = sb("sp", [128, 512], F32)
        EQ = sb("eq", [128, 512], F32)
        EK = sb("ek", [128, 512], F32)
        EKH = sb("ekh", [128, 512], F32)
        KTOK = sb("ktok", [128, 256], F32)
        KHS = sb("khs", [128, 512], BF16)
        VV = sb("vv", [128, 4, 2048], BF16)
        LNG = sb("lngb", [128, 2048], BF16)
        LNB = sb("lnbb", [128, 2048], BF16)
        VST = sb("vst", [128, 4, 512], BF16)
        UT = sb("ut", [128, 2, T], BF16)
        YSG = sb("ysg", [128, 2, T], BF16)
        XBs = [(XB.ap, [XB.res]), (arena_view(VV, 16384, F32), [VV.res])]
        XNs = [(XN.ap, [XN.res]), (arena_view(VST, 8192, BF16), [VST.res, UT.res, YSG.res])]
        LNS = sb("lns", [128, 16], F32)
        DECT = sb("dect", [128, 4], F32)
        LNTMP = Tl(XB[:, 0:2048], "lntmp")
        LNTMP.res = XB.res
        dma(LNTMP, LNTMP.ap, None, lng_d.partition_broadcast(128), part=False)
        P.add("dve", lambda e: e.tensor_copy(out=LNG.ap, in_=LNTMP.ap), reads=[LNTMP.res], writes=[LNG.res])
        dma(LNTMP, LNTMP.ap, None, lnb_d.partition_broadcast(128), part=False)
        P.add("dve", lambda e: e.tensor_copy(out=LNB.ap, in_=LNTMP.ap), reads=[LNTMP.res], writes=[LNB.res])
        for d in range(2):
            P.add("pool", lambda e, d=d: e.memset(SST[d].ap, 0.0), writes=[SST[d].res])

        wstate = {"n": 0}

        wscr = {"mode": None, "blk": 0}

        def load_w(src_ap_list, shape3):
            wb = WB[wstate["n"] % NWB]
            wstate["n"] += 1
            view = wb.ap.rearrange("p (a b) -> p a b", a=shape3[0])
            if wscr["mode"] == "rd":
                blk = wscr["blk"]
                wscr["blk"] += 1
                for hf in range(2):
                    o = wb.ap[:, hf * 4096:(hf + 1) * 4096]
                    WSCR = WSCRS[blk // 72]
                    src = WSCR[blk % 72, :, hf * 4096:(hf + 1) * 4096]
                    P.add("sp", lambda e, o=o, src=src: e.dma_start(out=o, in_=src), reads=[WSCR.res],
                          writes=[wb.res] if hf == 0 else [], parts=[wb.res] if hf == 1 else [], key=wb.res.name + '_h')
                return wb, view
            first = True
            for (dst_sl, src) in src_ap_list:
                o = view[:, dst_sl[0], dst_sl[1]]
                P.add("pool", lambda e, o=o, src=src: e.dma_start(out=o, in_=src),
                      writes=[wb.res] if first else [], parts=[] if first else [wb.res], key=wb.res.name)
                first = False
            if wscr["mode"] == "wb":
                blk = wscr["blk"]
                wscr["blk"] += 1
                WSCR = WSCRS[blk // 72]
                dst = WSCR[blk % 72, :, :]
                P.add("sp", lambda e, dst=dst: e.dma_start(out=dst, in_=wb.ap), reads=[wb.res], parts=[WSCR.res], key="wbk_" + wb.res.name)
            return wb, view

        def wsrc(w, r0, r1, c0, c1):
            return w[r0:r1, c0:c1].rearrange("(c p) n -> p c n", p=128)

        def load_fm(w, c0, ncols=256):
            return load_w([((slice(0, 16), slice(0, ncols)), wsrc(w, 0, 2048, c0, c0 + ncols)),
                           ((slice(16, 32), slice(0, ncols)), wsrc(w, 2048, 4096, c0, c0 + ncols))], (32, 256))

        def load_tm(w, r0, c0):
            return load_w([((slice(0, 8), slice(0, 512)), wsrc(w, r0, r0 + 1024, c0, c0 + 512)),
                           ((slice(8, 16), slice(0, 512)), wsrc(w, r0 + 1024, r0 + 2048, c0, c0 + 512))], (16, 512))

        def norm_mod_T(x_dram_rows, ntb, A, SH, dstT):
            modres = [MODT.res, MODCT.res, A1.res, A1C.res, A2.res]

            def front(tb):
                xb, xbr = XBs[tb % 2]
                xn, xnr = XNs[tb % 2]
                ss = SSs[tb]
                P.add("sp", lambda e: e.dma_start(out=xb, in_=x_dram_rows(tb)), writes=xbr, key="xb%d" % (tb % 2))
                P.add("dve", lambda e: e.memset(ss[:, 0:1], 0.0), writes=[ss.res])
                act(JUNK, xb, AF.Square, xbr + [ss.res], [STG.res], accum=ss[:, 0:1], parts=[ss.res])
                act(ss[:, 1:2], ss[:, 0:1], AF.Ln, [ss.res, CV.res], [], scale=1.0 / D, bias=EPSC, parts=[ss.res])
                act(ss[:, 1:2], ss[:, 1:2], AF.Exp, [ss.res], [], scale=-0.5, parts=[ss.res])
                P.add("dve", lambda e: e.tensor_scalar(out=xn, in0=xb, scalar1=ss[:, 1:2], scalar2=None, op0=ALU.mult),
                      reads=xbr + [ss.res], writes=xnr)

            def back(tb):
                xn, xnr = XNs[tb % 2]
                for k2 in range(KC // 8):
                    bi = 6 + (k2 % 2)
                    for j in range(8):
                        kc = k2 * 8 + j
                        P.add("pe", lambda e, bi=bi, j=j, kc=kc: e.transpose(PBH[bi][:, j * 128:(j + 1) * 128], xn[:, kc * 128:(kc + 1) * 128], IDB.ap),
                              reads=xnr + [IDB.res], parts=[PB[bi].res])
                    for j in range(8):
                        kc = k2 * 8 + j
                        if k2 % 2 == 0 or os.environ.get('KNOACT'):
                            P.add("dve", lambda e, bi=bi, j=j, kc=kc: e.tensor_scalar(
                                out=dstT[:, kc, tb * 128:(tb + 1) * 128], in0=PBH[bi][:, j * 128:(j + 1) * 128], scalar1=A[:, kc:kc + 1], scalar2=SH[:, kc:kc + 1],
                                op0=ALU.mult, op1=ALU.add), reads=modres, writes=[PB[bi].res], parts=[HXT.res])
                        else:
                            act(dstT[:, kc, tb * 128:(tb + 1) * 128], PBH[bi][:, j * 128:(j + 1) * 128], AF.Identity, modres, [PB[bi].res],
                                scale=A[:, kc:kc + 1], bias=SH[:, kc:kc + 1], parts=[HXT.res])

            JUNK = arena_view(STG, 8192, BF16)
            front(0)
            if ntb > 1:
                front(1)
            for tb in range(ntb):
                back(tb)
                if tb + 2 < ntb:
                    front(tb + 2)

        def mm_fm(view, j, bank, ntok, wb, hx=None):
            hx = HXT if hx is None else hx
            for kc in range(KC):
                P.add("pe", lambda e, kc=kc: e.matmul(bank[:, 0:ntok], lhsT=view[:, kc, j * 128:(j + 1) * 128], rhs=hx[:, kc, 0:ntok],
                                                      start=(kc == 0), stop=(kc == KC - 1)),
                      reads=[hx.res, wb.res], parts=[bank.res])

        def decay_and_state(pair, c, tile_tok0, rope, own, dirs, gchunk):
            c0 = c * 128
            ab = PB[2]
            P.add("pe", lambda e: e.matmul(ab.ap, lhsT=LRT[:, c0:c0 + 128], rhs=WDEC[:, pair, :], start=True, stop=True),
                  reads=[LRT.res, WDEC.res], parts=[ab.res])
            act(SP.ap, ab.ap, AF.Exp, [], [SP.res, ab.res], scale=-1.0)
            act(SP.ap, SP.ap, AF.Ln, [SP.res, CV.res], [SP.res], bias=ONE)
            bb = PB[3]
            for d in range(2):
                for hh in range(2):
                    blk = d * 2 + hh
                    P.add("pe", lambda e, d=d, blk=blk: e.matmul(bb[:, blk * 128:(blk + 1) * 128], lhsT=SP[:, blk * 128:(blk + 1) * 128], rhs=MK[d],
                                                                   start=(blk == 0), stop=(blk == 3), skip_group_check=True),
                          reads=[SP.res, CONSTS.res], parts=[bb.res])
            act(EQ.ap, bb.ap, AF.Exp, [], [EQ.res, bb.res], scale=-1.0 / 16)
            if own:
                act(EK.ap, bb.ap, AF.Exp, [], [EK.res, bb.res], scale=1.0 / 16)
            xb_ = PB[4]
            for d in range(2):
                P.add("pe", lambda e, d=d: e.matmul(xb_[:, d * 256:(d + 1) * 256], lhsT=MK[2 + d], rhs=SP[:, d * 256:(d + 1) * 256],
                                                    start=(d == 0), stop=(d == 1), skip_group_check=True),
                      reads=[SP.res, CONSTS.res], parts=[xb_.res])
            act(EKH.ap, xb_.ap, AF.Exp, [], [EKH.res, xb_.res], scale=-1.0 / 16)
            kt = PB[5]
            for hh in range(2):
                P.add("pe", lambda e, hh=hh: e.transpose(kt[:, hh * 128:(hh + 1) * 128], QKR[:, 2 + hh, c0:c0 + 128], IDF),
                      reads=[QKR.res, CONSTS.res], parts=[kt.res])
            P.add("dve", lambda e: e.tensor_copy(out=KTOK.ap, in_=kt[:, 0:256]), writes=[KTOK.res, kt.res])
            for d in range(2):
                P.add("dve", lambda e, d=d: e.tensor_tensor(out=KHS[:, d * 256:(d + 1) * 256], in0=KTOK.ap, in1=EKH[:, d * 256:(d + 1) * 256], op=ALU.mult),
                      reads=[KTOK.res, EKH.res], parts=[KHS.res])
            if own:
                for d in range(2):
                    last = 127 if d == 0 else 0
                    ev = EQ.ap.rearrange("p (b i) -> p b i", b=4)
                    P.add("pool", lambda e, d=d, ev=ev, last=last: e.tensor_copy(out=DEC[:, d, 2 * pair:2 * pair + 2, gchunk], in_=ev[:, 2 * d:2 * d + 2, last]),
                          reads=[EQ.res], parts=[DEC.res])
                    P.add("dve", lambda e, d=d, ev=ev: e.tensor_tensor(out=STG[:, 2 * d:2 * d + 2, c0:c0 + 128], in0=QKR[:, 0:2, c0:c0 + 128],
                                                                       in1=ev[:, 2 * d:2 * d + 2, :], op=ALU.mult),
                          reads=[QKR.res, EQ.res], parts=[STG.res])
                    ekv = EK.ap.rearrange("p (b i) -> p b i", b=4)
                    P.add("dve", lambda e, d=d, ekv=ekv: e.tensor_tensor(out=STG[:, 4 + 2 * d:6 + 2 * d, c0:c0 + 128], in0=QKR[:, 2:4, c0:c0 + 128],
                                                                         in1=ekv[:, 2 * d:2 * d + 2, :], op=ALU.mult),
                          reads=[QKR.res, EK.res], parts=[STG.res])
                    dma(KH[d], KH[d][2 * pair:2 * pair + 2, :, gchunk, :].rearrange("h p c -> p h c"), KHS,
                        KHS[:, d * 256:(d + 1) * 256].rearrange("p (h c) -> p h c", h=2), key="spill")
            return EQ, KHS

        def phase1_tile(kind, ti):
            ntok = NCTX if kind == "ctx" else T
            ntb = ntok // 128
            own = kind == "own"
            if kind == "own":
                xr = lambda tb: x_own[ti * T + tb * 128: ti * T + (tb + 1) * 128, :]
                A, SH = A1, SH1
            elif kind == "pre":
                xr = lambda tb: x_pre[ti * T + tb * 128: ti * T + (tb + 1) * 128, :]
                A, SH = A1, SH1
            else:
                xr = lambda tb: x_ctx[tb * 128:(tb + 1) * 128, :]
                A, SH = A1C, SH1C
            norm_mod_T(xr, ntb, A.ap, SH, HXT.ap)
            rope = kind != "ctx"
            if rope:
                src = cs_own_d if own else cs_pre_d
                dma(CS, CS.ap, None, src[:, :, ti * T:(ti + 1) * T], part=False)
            lb = PB[0]
            for kc in range(KC):
                P.add("pe", lambda e, kc=kc: e.matmul(lb[0:32, 0:ntok], lhsT=WLR[:, kc, :], rhs=HXT[:, kc, 0:ntok], start=(kc == 0), stop=(kc == KC - 1)),
                      reads=[HXT.res, WLR.res], parts=[lb.res])
            P.add("dve", lambda e: e.tensor_copy(out=LRT[0:32, 0:ntok], in_=lb[0:32, 0:ntok]), writes=[LRT.res, lb.res])
            if not own:
                VT = VV
                for vb in range(4):
                    for half in range(2):
                        wb, view = load_tm(w_in, half * 2048, V0 + vb * 512)
                        for tb in range(ntb):
                            bank = PB[tb % 2] if False else PB[tb]
                            for k16 in range(16):
                                kc = half * 16 + k16
                                P.add("pe", lambda e, bank=bank, view=view, tb=tb, kc=kc, k16=k16: e.matmul(
                                    bank.ap, lhsT=HXT[:, kc, tb * 128:(tb + 1) * 128], rhs=view[:, k16, :], start=(kc == 0), stop=(kc == KC - 1),
                                    skip_group_check=True), reads=[HXT.res, wb.res], parts=[bank.res])
                    for tb in range(ntb):
                        bank = PB[tb]
                        act(VT[:, tb, vb * 512:(vb + 1) * 512], bank.ap, AF.Copy, [], [bank.res], parts=[VT.res])
            if not own:
                dirs = (0, 1) if kind == "ctx" else (1,)
                wsp = 512 if kind == "ctx" else 256
                coff = 0 if kind == "ctx" else 256
                SP2 = [arena_view(STG, 8192, F32)[:, par * 1024:(par + 1) * 1024].rearrange("p (c w) -> p c w", c=ntb) for par in range(2)]
                KTOK4 = arena_view(SP, 4096, F32).rearrange("p (c w) -> p c w", c=4)
                KHS4 = arena_view(EK, 2048, BF16).rearrange("p (c w) -> p c w", c=4)

                def dsl(d):
                    return slice(d * 256, (d + 1) * 256) if kind == "ctx" else slice(0, 256)

                def stageA(p):
                    par = p % 2
                    for c in range(ntb):
                        bank = PB[4 + c % 2]
                        P.add("pe", lambda e, bank=bank, c=c: e.matmul(bank[:, 0:wsp], lhsT=LRT[:, c * 128:(c + 1) * 128], rhs=WDEC[:, p, coff:coff + wsp],
                                                                      start=True, stop=True), reads=[LRT.res, WDEC.res], parts=[bank.res])
                        act(SP2[par][:, c, :], bank[:, 0:wsp], AF.Exp, [], [bank.res], scale=-1.0, parts=[STG.res])
                    act(SP2[par], SP2[par], AF.Ln, [STG.res, CV.res], [], bias=ONE, parts=[STG.res])

                def kblock(p):
                    wb2, view2 = load_fm(w_in, K0 + p * 256)
                    for hh in range(2):
                        qi = 2 * (p % 2) + hh
                        bank = PB[hh]
                        mm_fm(view2, hh, bank, ntok, wb2)
                        if rope:
                            act(QRAW[:, 0:ntok], bank[:, 0:ntok], AF.Copy, [], [QRAW.res, bank.res])
                            rp = PB[4 + hh]
                            P.add("pe", lambda e, rp=rp: e.matmul(rp[:, 0:ntok], lhsT=ROPEP, rhs=QRAW[:, 0:ntok], start=True, stop=True),
                                  reads=[QRAW.res, CONSTS.res], parts=[rp.res])
                            P.add("dve", lambda e: e.tensor_tensor(out=RT1[:, 0:ntok], in0=QRAW[:, 0:ntok], in1=CS[:, 0, 0:ntok], op=ALU.mult),
                                  reads=[QRAW.res, CS.res], writes=[RT1.res])
                            P.add("dve", lambda e, rp=rp, qi=qi: e.tensor_tensor(out=QKR[:, qi, 0:ntok], in0=rp[:, 0:ntok], in1=CS[:, 1, 0:ntok], op=ALU.mult),
                                  reads=[CS.res], parts=[QKR.res], writes=[rp.res])
                            P.add("dve", lambda e, qi=qi: e.tensor_tensor(out=QKR[:, qi, 0:ntok], in0=QKR[:, qi, 0:ntok], in1=RT1[:, 0:ntok], op=ALU.add),
                                  reads=[QKR.res, RT1.res], parts=[QKR.res])
                        else:
                            act(QKR[:, qi, 0:ntok], bank[:, 0:ntok], AF.Copy, [], [bank.res], parts=[QKR.res])

                def stageB(p):
                    par = p % 2
                    kt = PB[2]
                    for c in range(ntb):
                        for hh in range(2):
                            qi = 2 * par + hh
                            P.add("pe", lambda e, hh=hh, qi=qi, c=c: e.transpose(kt[:, hh * 128:(hh + 1) * 128], QKR[:, qi, c * 128:(c + 1) * 128], IDF),
                                  reads=[QKR.res, CONSTS.res], parts=[kt.res])
                        P.add("dve", lambda e, c=c: e.tensor_copy(out=KTOK4[:, c, :], in_=kt[:, 0:256]), writes=[kt.res], parts=[SP.res, EQ.res])
                    for d in dirs:
                        for jb in range(ntb):
                            xb_ = PB[3]
                            after = [mb for mb in range(ntb) if (mb > jb if d == 0 else mb < jb)]
                            P.add("pe", lambda e, d=d, jb=jb, after=after: e.matmul(xb_[:, 0:256], lhsT=MK[2 + d], rhs=SP2[par][:, jb, dsl(d)],
                                                                                  start=True, stop=(not after)),
                                  reads=[STG.res, CONSTS.res], parts=[xb_.res])
                            for i_, mb in enumerate(after):
                                P.add("pe", lambda e, d=d, mb=mb, i_=i_, after=after: e.matmul(xb_[:, 0:256], lhsT=ONESF.ap, rhs=SP2[par][:, mb, dsl(d)],
                                                                                              start=False, stop=(i_ == len(after) - 1)),
                                      reads=[STG.res, ONESF.res], parts=[xb_.res])
                            act(EKH[:, 0:256], xb_[:, 0:256], AF.Exp, [], [EKH.res, xb_.res], scale=-1.0 / 16)
                            P.add("dve", lambda e, jb=jb: e.tensor_tensor(out=KHS4[:, jb, :], in0=KTOK4[:, jb, :], in1=EKH[:, 0:256], op=ALU.mult),
                                  reads=[SP.res, EQ.res, EKH.res], parts=[EK.res])
                        db = PB[2]
                        for hh in range(2):
                            for c in range(ntb):
                                sl = slice(dsl(d).start + hh * 128, dsl(d).start + (hh + 1) * 128)
                                P.add("pe", lambda e, hh=hh, c=c, sl=sl: e.matmul(db[:, 256 + hh:257 + hh], lhsT=SP2[par][:, c, sl], rhs=ONESF[:, 0:1],
                                                                                 start=(c == 0), stop=(c == ntb - 1), skip_group_check=True),
                                      reads=[STG.res, ONESF.res], parts=[db.res])
                        act(DECT[:, 0:2], db[:, 256:258], AF.Exp, [], [DECT.res, db.res], scale=-1.0 / 16)
                        for hh in range(2):
                            h = 2 * p + hh
                            ub = PB[6 + hh]
                            for c in range(ntb):
                                P.add("pe", lambda e, ub=ub, hh=hh, c=c, h=h: e.matmul(
                                    ub[:, 0:256], lhsT=KHS4[:, c, hh * 128:(hh + 1) * 128], rhs=VV[:, c, h * 256:(h + 1) * 256],
                                    start=(c == 0), stop=(c == ntb - 1)), reads=[EK.res, VV.res], parts=[ub.res])
                            P.add("dve", lambda e, ub=ub, d=d, h=h, hh=hh: e.scalar_tensor_tensor(
                                out=SST[d][:, h, :], in0=SST[d][:, h, :], scalar=DECT[:, hh:hh + 1], in1=ub[:, 0:256],
                                op0=ALU.mult, op1=ALU.add), reads=[DECT.res, SST[d].res], writes=[ub.res], parts=[SST[d].res])

                stageA(0)
                kblock(0)
                for p in range(1, 4):
                    stageA(p)
                    kblock(p)
                    stageB(p - 1)
                stageB(3)
                return
            for pair in range(4):
                wb, view = load_fm(w_in, Q0 + pair * 256)
                blocks = [(wb, view, 0), (wb, view, 1)]
                wb2, view2 = load_fm(w_in, K0 + pair * 256)
                blocks += [(wb2, view2, 0), (wb2, view2, 1)]
                for (wbx, vw, j), qi in zip(blocks, [0, 1, 2, 3]):
                    bank = PB[qi % 2]
                    mm_fm(vw, j, bank, ntok, wbx)
                    scale = (128.0 ** -0.5) if qi < 2 else 1.0
                    act(QRAW[:, 0:ntok], bank[:, 0:ntok], AF.Copy, [], [QRAW.res, bank.res], scale=scale)
                    rp = PB[6 + qi % 2]
                    P.add("pe", lambda e, rp=rp: e.matmul(rp[:, 0:ntok], lhsT=ROPEP, rhs=QRAW[:, 0:ntok], start=True, stop=True),
                          reads=[QRAW.res, CONSTS.res], parts=[rp.res])
                    P.add("dve", lambda e: e.tensor_tensor(out=RT1[:, 0:ntok], in0=QRAW[:, 0:ntok], in1=CS[:, 0, 0:ntok], op=ALU.mult),
                          reads=[QRAW.res, CS.res], writes=[RT1.res])
                    P.add("dve", lambda e, rp=rp, qi=qi: e.tensor_tensor(out=QKR[:, qi, 0:ntok], in0=rp[:, 0:ntok], in1=CS[:, 1, 0:ntok], op=ALU.mult),
                          reads=[CS.res], parts=[QKR.res], writes=[rp.res])
                    P.add("dve", lambda e, qi=qi: e.tensor_tensor(out=QKR[:, qi, 0:ntok], in0=QKR[:, qi, 0:ntok], in1=RT1[:, 0:ntok], op=ALU.add),
                          reads=[QKR.res, RT1.res], parts=[QKR.res])
                for c in range(ntb):
                    decay_and_state(pair, c, ti * T, rope, True, (0, 1), ti * 4 + c)
                for qk in range(2):
                    for d in range(2):
                        dma(QD[qk][d], QD[qk][d][2 * pair:2 * pair + 2, :, ti * T:(ti + 1) * T].rearrange("h p t -> p h t"), STG,
                            STG[:, qk * 4 + 2 * d: qk * 4 + 2 * d + 2, :], key="spill")
            for (c0col, dst, fn) in ((V0, VD, AF.Copy), (R0, RD, AF.Silu)):
                for vb in range(4):
                    for half in range(2):
                        wb, view = load_tm(w_in, half * 2048, c0col + vb * 512)
                        for tb in range(4):
                            bank = PB[tb]
                            for k16 in range(16):
                                kc = half * 16 + k16
                                P.add("pe", lambda e, bank=bank, view=view, tb=tb, kc=kc, k16=k16: e.matmul(
                                    bank.ap, lhsT=HXT[:, kc, tb * 128:(tb + 1) * 128], rhs=view[:, k16, :], start=(kc == 0), stop=(kc == KC - 1),
                                    skip_group_check=True), reads=[HXT.res, wb.res], parts=[bank.res])
                    for tb in range(4):
                        bank = PB[tb]
                        act(VST[:, tb, :], bank.ap, fn, [], [bank.res], parts=[VST.res])
                    dma(dst, dst[ti * T:(ti + 1) * T, vb * 512:(vb + 1) * 512].rearrange("(t p) n -> p t n", p=128), VST, VST.ap, key="spill")
            P.add("dve", lambda e: e.memset(LNS.ap, 0.0), writes=[LNS.res])
            for vb in range(4):
                for half in range(2):
                    wb, view = load_tm(w_in, half * 2048, VV0 + vb * 512)
                    for tb in range(4):
                        bank = PB[tb]
                        for k16 in range(16):
                            kc = half * 16 + k16
                            P.add("pe", lambda e, bank=bank, view=view, tb=tb, kc=kc, k16=k16: e.matmul(
                                bank.ap, lhsT=HXT[:, kc, tb * 128:(tb + 1) * 128], rhs=view[:, k16, :], start=(kc == 0), stop=(kc == KC - 1),
                                skip_group_check=True), reads=[HXT.res, wb.res], parts=[bank.res])
                for tb in range(4):
                    bank = PB[tb]
                    act(VV[:, tb, vb * 512:(vb + 1) * 512], bank.ap, AF.Gelu, [], [bank.res], parts=[VV.res])
            for tb in range(4):
                P.add("dve", lambda e, tb=tb: e.tensor_reduce(out=LNS[:, tb:tb + 1], in_=VV[:, tb, :], axis=AX.X, op=ALU.add),
                      reads=[VV.res], parts=[LNS.res])
                act(LNTMP.ap, VV[:, tb, :], AF.Square, [VV.res, LNS.res], [LNTMP.res], accum=LNS[:, 4 + tb:5 + tb], parts=[LNS.res])
            P.add("dve", lambda e: e.tensor_scalar(out=LNS[:, 0:4], in0=LNS[:, 0:4], scalar1=1.0 / 2048, scalar2=None, op0=ALU.mult),
                  reads=[LNS.res], writes=[LNS.res])
            P.add("dve", lambda e: e.tensor_tensor(out=LNS[:, 8:12], in0=LNS[:, 0:4], in1=LNS[:, 0:4], op=ALU.mult), reads=[LNS.res], writes=[LNS.res])
            P.add("dve", lambda e: e.scalar_tensor_tensor(out=LNS[:, 4:8], in0=LNS[:, 4:8], scalar=1.0 / 2048, in1=LNS[:, 8:12], op0=ALU.mult, op1=ALU.subtract),
                  reads=[LNS.res], writes=[LNS.res])
            act(LNS[:, 4:8], LNS[:, 4:8], AF.Ln, [LNS.res, CV.res], [LNS.res], bias=EPSC)
            act(LNS[:, 4:8], LNS[:, 4:8], AF.Exp, [LNS.res], [LNS.res], scale=-0.5)
            P.add("dve", lambda e: e.scalar_tensor_tensor(out=LNS[:, 8:12], in0=LNS[:, 0:4], scalar=-1.0, in1=LNS[:, 4:8], op0=ALU.mult, op1=ALU.mult),
                  reads=[LNS.res], writes=[LNS.res])
            for tb in range(4):
                act(LNTMP.ap, VV[:, tb, :], AF.Identity, [VV.res, LNS.res], [LNTMP.res], scale=LNS[:, 4 + tb:5 + tb], bias=LNS[:, 8 + tb:9 + tb])
                P.add("dve", lambda e: e.tensor_tensor(out=LNTMP.ap, in0=LNTMP.ap, in1=LNG.ap, op=ALU.mult), reads=[LNTMP.res, LNG.res], writes=[LNTMP.res])
                P.add("dve", lambda e, tb=tb: e.tensor_tensor(out=VV[:, tb, :], in0=LNTMP.ap, in1=LNB.ap, op=ALU.add), reads=[LNTMP.res, LNB.res], parts=[VV.res])
            for ub_ in range(8):
                wb, view = load_fm(w_in, U0 + ub_ * 256)
                for j in range(2):
                    cc = ub_ * 2 + j
                    g = cc // 4
                    bank = PB[j]
                    mm_fm(view, j, bank, T, wb)
                    act(UT[:, j, :], bank.ap, AF.Gelu, [], [bank.res], parts=[UT.res])
                    sbk = PB[2 + j]
                    P.add("pe", lambda e, sbk=sbk, g=g: e.matmul(sbk.ap, lhsT=ONESB[0:2, :], rhs=BS2[0:2, g, :], start=True, stop=False, skip_group_check=True),
                          reads=[ONESB.res, BS2.res], parts=[sbk.res])
                    for tb in range(4):
                        P.add("pe", lambda e, sbk=sbk, tb=tb, cc=cc, g=g: e.matmul(sbk[:, tb * 128:(tb + 1) * 128], lhsT=VV[:, tb, cc * 128:(cc + 1) * 128],
                                                                                 rhs=WST[:, g, :], start=False, stop=(tb == 3), skip_group_check=True),
                              reads=[VV.res, WST.res], parts=[sbk.res])
                    P.add("dve", lambda e, sbk=sbk, j=j: e.tensor_tensor(out=YSG[:, j, :], in0=sbk.ap, in1=UT[:, j, :], op=ALU.mult),
                          reads=[UT.res], writes=[sbk.res], parts=[YSG.res])
                dma(CATT, CATT[2048 + ub_ * 256: 2048 + (ub_ + 1) * 256, ti * T:(ti + 1) * T].rearrange("(j p) t -> p j t", p=128), YSG, YSG.ap, key="spill")

        phase1_tile("ctx", 0)
        for ti in range(NPRE // T - 1, -1, -1):
            phase1_tile("pre", ti)
        for ti in range(NT // T):
            phase1_tile("own", ti)
        P.barrier()
        release(m1)

        m2 = mark()
        GLAG = sb("glagb", [128, 2048], F32)
        dma(GLAG, GLAG.ap, None, glag_d.partition_broadcast(128), part=False)
        HQ2 = [[[sb(f"hq{b}{qk}{d}", [128, NT], BF16) for d in range(2)] for qk in range(2)] for b in range(2)]
        HKH2 = [[sb(f"hkh{b}{d}", [128, NCH, 128], BF16) for d in range(2)] for b in range(2)]
        HV2 = [sb(f"hv{b}", [128, NCH, 256], BF16) for b in range(2)]
        HR2 = [sb(f"hr{b}", [128, NCH, 256], BF16) for b in range(2)]

        def load_head(h):
            HQ, HKH, HV, HR = HQ2[h % 2], HKH2[h % 2], HV2[h % 2], HR2[h % 2]
            for qk in range(2):
                for d in range(2):
                    dma(HQ[qk][d], HQ[qk][d].ap, QD[qk][d], QD[qk][d][h, :, :], part=False)
            for d in range(2):
                dma(HKH[d], HKH[d].ap, KH[d], KH[d][h, :, :, :], part=False)
            dma(HV, HV.ap, VD, VD[:, h * 256:(h + 1) * 256].rearrange("(c p) n -> p c n", p=128), part=False)
            dma(HR, HR.ap, RD, RD[:, h * 256:(h + 1) * 256].rearrange("(c p) n -> p c n", p=128), part=False)

        OST = sb("ost", [128, NCH, 256], F32)
        YCT = sb("yct", [128, 2, NT], BF16)
        ATT = [sb(f"att{d}", [128, 128], BF16) for d in range(2)]
        OSUM = [sb(f"osum{d}", [128, 256], F32) for d in range(2)]
        YT = [sb(f"yt{d}", [128, 256], BF16) for d in range(2)]
        JNK = [sb(f"jnk{d}", [128, 256], BF16) for d in range(2)]
        RS = [sb(f"rs{d}", [128, 4], F32) for d in range(2)]
        def ada_pump(n):
            return

        SBA = [sb(f"sba{d}", [128, NCH, 256], BF16) for d in range(2)]
        def head_body(h, HQ, HKH, HV, HR):
                for s_ in range(NCH):
                    for d in range(2):
                        c = s_ if d == 0 else NCH - 1 - s_
                        act(SBA[d][:, c, :], SST[d][:, h, :], AF.Copy, [SST[d].res], [], parts=[SBA[d].res])
                        if s_ == NCH - 1:
                            continue
                        ub = PB[(2 * s_ + d) % 6]
                        P.add("pe", lambda e, ub=ub, d=d, c=c: e.matmul(ub[:, 0:256], lhsT=HKH[d][:, c, :], rhs=HV[:, c, :], start=True, stop=True),
                              reads=[HKH[d].res, HV.res], parts=[ub.res])
                        P.add("dve", lambda e, ub=ub, d=d, c=c, h=h: e.scalar_tensor_tensor(
                            out=SST[d][:, h, :], in0=SST[d][:, h, :], scalar=DEC[:, d, h, c:c + 1], in1=ub[:, 0:256], op0=ALU.mult, op1=ALU.add),
                            reads=[DEC.res, SST[d].res], writes=[ub.res], parts=[SST[d].res])
                for s_ in range(NCH):
                    ada_pump((16 + NCH - 1) // NCH)
                    cc = [s_, NCH - 1 - s_]
                    css = [slice(c * 128, (c + 1) * 128) for c in cc]
                    for d in range(2):
                        ab = PB[d]
                        P.add("pe", lambda e, ab=ab, d=d, cs=css[d]: e.matmul(ab[:, 0:128], lhsT=HQ[1][d][:, cs], rhs=HQ[0][d][:, cs], start=True, stop=True),
                              reads=[HQ[0][d].res, HQ[1][d].res], parts=[ab.res])
                    for d in range(2):
                        ab = PB[d]
                        P.add("dve", lambda e, ab=ab, d=d: e.tensor_tensor(out=ATT[d].ap, in0=ab[:, 0:128], in1=MK[d], op=ALU.mult),
                              reads=[CONSTS.res], writes=[ATT[d].res, ab.res])
                    for d in range(2):
                        ob = PB[2 + d]
                        P.add("pe", lambda e, ob=ob, d=d, c=cc[d]: e.matmul(ob[:, 0:256], lhsT=ATT[d].ap, rhs=HV[:, c, :], start=True, stop=False),
                              reads=[ATT[d].res, HV.res], parts=[ob.res])
                        P.add("pe", lambda e, ob=ob, d=d, cs=css[d], c=cc[d]: e.matmul(ob[:, 0:256], lhsT=HQ[0][d][:, cs], rhs=SBA[d][:, c, :], start=False, stop=True),
                              reads=[HQ[0][d].res, SBA[d].res], parts=[ob.res])
                    for d in range(2):
                        ob = PB[2 + d]
                        c = cc[d]
                        cs = css[d]
                        if s_ < NCH // 2:
                            act(OST[:, c, :], ob[:, 0:256], AF.Copy, [], [ob.res], parts=[OST.res])
                        else:
                            P.add("dve", lambda e, ob=ob, c=c, d=d: e.tensor_tensor(out=OSUM[d].ap, in0=ob[:, 0:256], in1=OST[:, c, :], op=ALU.add),
                                  reads=[OST.res], writes=[OSUM[d].res, ob.res])
                            P.add("dve", lambda e, d=d: e.memset(RS[d][:, 0:1], 0.0), writes=[RS[d].res])
                            act(JNK[d].ap, OSUM[d].ap, AF.Square, [OSUM[d].res, RS[d].res], [JNK[d].res], accum=RS[d][:, 0:1], parts=[RS[d].res])
                            act(RS[d][:, 1:2], RS[d][:, 0:1], AF.Ln, [RS[d].res, CV.res], [], scale=1.0 / 256, bias=EPSC, parts=[RS[d].res])
                            act(RS[d][:, 1:2], RS[d][:, 1:2], AF.Exp, [RS[d].res], [], scale=-0.5, parts=[RS[d].res])
                            P.add("dve", lambda e, h=h, d=d: e.scalar_tensor_tensor(out=OSUM[d].ap, in0=OSUM[d].ap, scalar=RS[d][:, 1:2], in1=GLAG[:, h * 256:(h + 1) * 256],
                                                                                     op0=ALU.mult, op1=ALU.mult), reads=[OSUM[d].res, RS[d].res, GLAG.res], writes=[OSUM[d].res])
                            P.add("dve", lambda e, c=c, d=d: e.tensor_tensor(out=YT[d].ap, in0=OSUM[d].ap, in1=HR[:, c, :], op=ALU.mult),
                                  reads=[OSUM[d].res, HR.res], writes=[YT[d].res])
                            tbk = 6
                            for b2 in range(2):
                                P.add("pe", lambda e, tbk=tbk, b2=b2, d=d: e.transpose(PBH[tbk][:, d * 256 + b2 * 128: d * 256 + (b2 + 1) * 128], YT[d][:, b2 * 128:(b2 + 1) * 128], IDB.ap),
                                      reads=[YT[d].res, IDB.res], parts=[PB[tbk].res])
                            P.add("dve", lambda e, tbk=tbk, cs=cs, d=d: e.tensor_copy(out=YCT[:, :, cs], in_=PBH[tbk][:, d * 256:(d + 1) * 256].rearrange("p (b c) -> p b c", b=2)),
                                  writes=[PB[tbk].res], parts=[YCT.res])
                dma(CATT, CATT[h * 256:(h + 1) * 256, :].rearrange("(b p) t -> p b t", p=128), YCT, YCT.ap, key="spill2")

        load_head(0)
        for h in range(8):
            if h + 1 < 8:
                load_head(h + 1)
            head_body(h, HQ2[h % 2], HKH2[h % 2], HV2[h % 2], HR2[h % 2])
        P.barrier()
        release(m2)

        release(base3_mark)
        m3 = mark()
        MM = sb("mm", [128, 4, D], F32)
        MMR = [Res(f"mmr{i}") for i in range(4)]
        SSR = [Res(f"ssr{i}") for i in range(4)]
        MM0 = Tl(MM[:, 0, :], "mm0")
        MM0.res = MMR[0]
        XB3 = sb("xblk3", [128, D], F32)
        H2T = sb("h2T", [128, KC, T], BF16)
        CT = sb("ct", [128, KC, T], BF16)
        HID = Tl(CT[:, 0:16, :], "hid")
        HID.res = CT.res
        XN3 = Tl(CT.ap.rearrange("p a b -> p (a b)")[:, 8192:8192 + D], "xn3")
        XN3.res = CT.res
        WB[:] = [sb(f"wb3{i}", [128, 8192], BF16) for i in range(NWB)]
        G1B = sb("g1b", [128, D], BF16)
        G2B = sb("g2b", [128, D], BF16)
        RT13 = sb("rt13", [128, T], F32)
        SS3 = sb("ss3", [128, 8], F32)
        for (G, goff, post) in ((G1B, 2 * D, post1_d), (G2B, 5 * D, post2_d)):
            dma(MM0, MM0.ap, MODROW, MODROW[0:1, goff:goff + D].partition_broadcast(128), part=False, key="mmld")
            dma(XB3, XB3.ap, None, post.partition_broadcast(128), part=False)
            P.add("dve", lambda e, G=G: e.tensor_tensor(out=G.ap, in0=MM[:, 0, :], in1=XB3.ap, op=ALU.mult), reads=[MMR[0], XB3.res], writes=[G.res])

        def rstd_of(src_ap, src_res, col, junk=None):
            jap, jres = (XN3.ap, [XN3.res]) if junk is None else junk
            ssr = SSR[col]
            P.add("dve", lambda e: e.memset(SS3[:, col:col + 1], 0.0), writes=[ssr])
            act(jap, src_ap, AF.Square, [src_res, ssr], jres, accum=SS3[:, col:col + 1], parts=[ssr])
            act(SS3[:, 4 + col:5 + col], SS3[:, col:col + 1], AF.Ln, [ssr, CV.res], [], scale=1.0 / D, bias=EPSC, parts=[ssr])
            act(SS3[:, 4 + col:5 + col], SS3[:, 4 + col:5 + col], AF.Exp, [ssr], [], scale=-0.5, parts=[ssr])
            return SS3[:, 4 + col:5 + col]

        def tm_matmul(w, nhalf, rowbase, colbase, lhs_of, first, last):
            for half in range(nhalf):
                wb, view = load_tm(w, rowbase + half * 2048, colbase)
                for tb in range(4):
                    bank = PB[tb]
                    for k16 in range(16):
                        kk = half * 16 + k16
                        P.add("pe", lambda e, bank=bank, view=view, tb=tb, kk=kk, k16=k16: e.matmul(
                            bank.ap, lhsT=lhs_of(kk, tb), rhs=view[:, k16, :], start=(first and kk == 0), stop=(last and kk == nhalf * 16 - 1),
                            skip_group_check=True), reads=[H2T.res, CT.res, wb.res], parts=[bank.res])

        for ti in range(NT // T):
            wscr["mode"] = ("wb" if ti == 0 else "rd") if NT // T > 1 else None
            wscr["blk"] = 0
            dma(CT, CT.ap, CATT, CATT[:, ti * T:(ti + 1) * T].rearrange("(c p) t -> p c t", p=128), part=False)
            for cb in range(8):
                tm_matmul(w_o, 2, 0, cb * 512, lambda kk, tb: CT[:, kk, tb * 128:(tb + 1) * 128], True, True)
                for tb in range(4):
                    act(MM[:, tb, cb * 512:(cb + 1) * 512], PB[tb].ap, AF.Copy, [], [PB[tb].res], parts=[MMR[tb]])
            for tb in range(4):
                r0 = ti * T + tb * 128
                rs = rstd_of(MM[:, tb, :], MMR[tb], tb)
                dma(XB3, XB3.ap, None, x_own[r0:r0 + 128, :], part=False)
                P.add("dve", lambda e, tb=tb, rs=rs: e.scalar_tensor_tensor(out=MM[:, tb, :], in0=MM[:, tb, :], scalar=rs, in1=G1B.ap, op0=ALU.mult, op1=ALU.mult),
                      reads=[MMR[tb], SSR[tb], G1B.res], parts=[MMR[tb]])
                P.add("dve", lambda e, tb=tb: e.tensor_tensor(out=XB3.ap, in0=XB3.ap, in1=MM[:, tb, :], op=ALU.add), reads=[XB3.res, MMR[tb]], writes=[XB3.res])
                dma(X1, X1[r0:r0 + 128, :], XB3, XB3.ap, key="x1st")
                rs2 = rstd_of(XB3.ap, XB3.res, tb)
                P.add("dve", lambda e, rs2=rs2: e.tensor_scalar(out=XN3.ap, in0=XB3.ap, scalar1=rs2, scalar2=None, op0=ALU.mult),
                      reads=[XB3.res, SSR[tb]], writes=[XN3.res])
                for k2 in range(KC // 8):
                    bi = 6 + (k2 % 2)
                    for j in range(8):
                        kc = k2 * 8 + j
                        P.add("pe", lambda e, bi=bi, j=j, kc=kc: e.transpose(PBH[bi][:, j * 128:(j + 1) * 128], XN3[:, kc * 128:(kc + 1) * 128], IDB.ap),
                              reads=[XN3.res, IDB.res], parts=[PB[bi].res])
                    for j in range(8):
                        kc = k2 * 8 + j
                        P.add("dve", lambda e, bi=bi, j=j, kc=kc, tb=tb: e.tensor_scalar(
                            out=H2T[:, kc, tb * 128:(tb + 1) * 128], in0=PBH[bi][:, j * 128:(j + 1) * 128], scalar1=A2[:, kc:kc + 1], scalar2=SH2[:, kc:kc + 1],
                            op0=ALU.mult, op1=ALU.add), reads=[MODT.res, A2.res], writes=[PB[bi].res], parts=[H2T.res])
            for grp in range(8):
                for b8 in range(8):
                    wb, view = load_fm(w_1, grp * 2048 + b8 * 256)
                    for j in range(2):
                        hc = b8 * 2 + j
                        bank = PB[4 + j]
                        mm_fm(view, j, bank, T, wb, H2T)
                        act(RT13.ap, bank.ap, AF.Relu, [], [RT13.res, bank.res])
                        P.add("dve", lambda e, hc=hc: e.tensor_tensor(out=HID[:, hc, :], in0=RT13.ap, in1=RT13.ap, op=ALU.mult), reads=[RT13.res], parts=[HID.res])
                for cb in range(8):
                    tm_matmul(w_2, 1, grp * 2048, cb * 512, lambda kk, tb: HID[:, kk, tb * 128:(tb + 1) * 128], True, True)
                    for tb in range(4):
                        if grp == 0:
                            act(MM[:, tb, cb * 512:(cb + 1) * 512], PB[tb].ap, AF.Copy, [], [PB[tb].res], parts=[MMR[tb]])
                        else:
                            P.add("dve", lambda e, tb=tb, cb=cb: e.tensor_tensor(out=MM[:, tb, cb * 512:(cb + 1) * 512], in0=PB[tb].ap,
                                                                                 in1=MM[:, tb, cb * 512:(cb + 1) * 512], op=ALU.add),
                                  reads=[MMR[tb]], writes=[PB[tb].res], parts=[MMR[tb]])
            for tb in range(4):
                r0 = ti * T + tb * 128
                rs = rstd_of(MM[:, tb, :], MMR[tb], tb, junk=(arena_view(H2T, 8192, BF16), [H2T.res]))
                dma(XB3, XB3.ap, X1, X1[r0:r0 + 128, :], part=False)
                P.add("dve", lambda e, tb=tb, rs=rs: e.scalar_tensor_tensor(out=MM[:, tb, :], in0=MM[:, tb, :], scalar=rs, in1=G2B.ap, op0=ALU.mult, op1=ALU.mult),
                      reads=[MMR[tb], SSR[tb], G2B.res], parts=[MMR[tb]])
                P.add("dve", lambda e, tb=tb: e.tensor_tensor(out=XB3.ap, in0=XB3.ap, in1=MM[:, tb, :], op=ALU.add), reads=[XB3.res, MMR[tb]], writes=[XB3.res])
                dma(y_out, y_out[r0:r0 + 128, :], XB3, XB3.ap, key="yst")
        P.add("sp", None, reads=[y_out.res])
        P.emit(nc, st)
    return nc


_NC_CACHE = {}


def _vecT(v):
    return np.ascontiguousarray(v.reshape(-1, 128).T)


def _rope_tables(pos):
    inv = (np.float32(10000.0) ** (-np.arange(32, dtype=np.float32) / np.float32(32))).astype(np.float32)
    row = (pos // 64).astype(np.float32)
    col = (pos % 64).astype(np.float32)
    ang = np.empty((128, pos.shape[0]), np.float32)
    for ch in range(128):
        base = row if ch < 64 else col
        ang[ch] = base * inv[ch % 32]
    out = np.empty((128, 2, pos.shape[0]), np.float32)
    out[:, 0] = np.cos(ang)
    out[:, 1] = np.sin(ang)
    return out


def _consts():
    c = np.zeros((128, 6, 128), np.float32)
    c[:, 0] = np.eye(128, dtype=np.float32)
    Pm = np.zeros((128, 128), np.float32)
    for base in (0, 64):
        for i in range(32):
            Pm[base + 32 + i, base + i] = -1.0
            Pm[base + i, base + 32 + i] = 1.0
    c[:, 1] = Pm
    j = np.arange(128)[:, None]
    i = np.arange(128)[None, :]
    c[:, 2] = (j <= i)
    c[:, 3] = (j >= i)
    c[:, 4] = (j > i)
    c[:, 5] = (j < i)
    return c


def make_in_maps(inputs):
    x = np.asarray(inputs["x"], np.float32)
    c = np.asarray(inputs["c"], np.float32)
    ctx = np.asarray(inputs["ctx"], np.float32)
    c_ctx = np.asarray(inputs["c_ctx"], np.float32)
    g = lambda k: np.asarray(inputs[k], np.float32)[0]
    w_ada, b_ada, w_in = g("w_ada"), g("b_ada"), g("w_in")
    w_o, w_1, w_2 = g("w_o"), g("w_1"), g("w_2")
    wdf, bdf, wdb, bdb = g("w_dec_f"), g("b_dec_f"), g("w_dec_b"), g("b_dec_b")
    w_s, b_s = g("w_s"), g("b_s")
    w_in_sw = w_in.copy()
    w_in_sw[:, LF0:LF0 + 16] = w_in[:, LB0:LB0 + 16]
    w_in_sw[:, LB0:LB0 + 16] = w_in[:, LF0:LF0 + 16]
    consts = _consts()
    pos_all = np.arange(2 * NT)
    shared = dict(w_o=w_o, w_1=w_1, w_2=w_2, w_ada=w_ada, bada=b_ada.reshape(1, -1),
                  pre1T=_vecT(g("pre1_g")), pre2T=_vecT(g("pre2_g")),
                  post1=g("post1_g").reshape(1, -1), post2=g("post2_g").reshape(1, -1),
                  glag=g("gla_norm_g").reshape(1, -1), lng=g("sg_ln_g").reshape(1, -1), lnb=g("sg_ln_b").reshape(1, -1),
                  consts=consts)
    maps = []
    for b in range(4):
        for h in range(2):
            m = dict(shared)
            if h == 0:
                xs, cx, pos = x[b], ctx[b], pos_all
                m["w_in"] = w_in
                wl, bl = (wdf, wdb), (bdf, bdb)
                ws, bs = w_s, b_s
            else:
                xs, cx, pos = x[b, ::-1], ctx[b, ::-1], pos_all[::-1]
                m["w_in"] = w_in_sw
                wl, bl = (wdb, wdf), (bdb, bdf)
                ws, bs = w_s[:, ::-1, ::-1], b_s[:, ::-1]
            m["x_own"] = np.ascontiguousarray(xs[:NT])
            m["x_pre"] = np.ascontiguousarray(xs[NT:])
            m["x_ctx"] = np.ascontiguousarray(cx)
            craw = np.empty((128, KC, 2), np.float32)
            craw[:, :, 0] = _vecT(c[b])
            craw[:, :, 1] = _vecT(c_ctx)
            m["craw"] = craw
            wd = np.zeros((33, 4, 512), np.float32)
            for p in range(4):
                for d in range(2):
                    for hh in range(2):
                        hd = 2 * p + hh
                        sl = slice((d * 2 + hh) * 128, (d * 2 + hh + 1) * 128)
                        wd[d * 16:(d + 1) * 16, p, sl] = wl[d][:, hd * 128:(hd + 1) * 128]
                        wd[32, p, sl] = bl[d][hd * 128:(hd + 1) * 128]
            m["wdec"] = wd
            m["wsT"] = np.ascontiguousarray(np.transpose(ws, (2, 0, 1)))
            m["bs"] = np.ascontiguousarray(np.tile(bs[:, None, :], (1, 4, 1)).reshape(1, 4, 512))
            m["cs_own"] = _rope_tables(pos[:NT])
            m["cs_pre"] = _rope_tables(pos[NT:])
            maps.append(m)
    return maps


def kernel(**inputs):
    if "nc" not in _NC_CACHE:
        _NC_CACHE["nc"] = build()
    nc = _NC_CACHE["nc"]
    maps = make_in_maps(inputs)
    res = run_bass_kernel_spmd(nc, maps, core_ids=list(range(8)))
    out = np.empty((4, 4096, D), np.float32)
    for b in range(4):
        for h in range(2):
            y = np.asarray(res.results[2 * b + h]["y"])
            if h == 0:
                out[b, :NT] = y
            else:
                out[b, NT:] = y[::-1]
    return out
```
